# Optimizing a Trainium2 kernel written in Bass

```python
import jax, jax.numpy as jnp
from jax import lax
import numpy as np


D_MODEL = 2048
BATCH = 2
SEQ = 4096
DEPTH = 2

D_MIX = D_MODEL
D_FOX = D_MIX // 2
D_HGRN = D_MIX - D_FOX
FOX_HEADS = 8
FOX_HEAD_DIM = D_FOX // FOX_HEADS
HGRN_EXPAND = 128
HGRN_HEADS = D_HGRN // HGRN_EXPAND
HGRN_DV = D_HGRN // HGRN_HEADS
D_FF = 5632
CONV_WIDTH = 3
PLE_DIM = 256
Q_BLOCK = 128
CHUNK = 64
EPS = 1e-6

OFF_FOX_K = D_FOX
OFF_FOX_V = 2 * D_FOX
OFF_FOX_F = 3 * D_FOX
OFF_HG_Q = OFF_FOX_F + FOX_HEADS
OFF_HG_F = OFF_HG_Q + D_HGRN
OFF_HG_I = OFF_HG_F + D_HGRN
OFF_HG_G = OFF_HG_I + D_HGRN
N_IN = OFF_HG_G + D_HGRN

kernel_name = 'fox_hgrn2_parallel_hybrid_block'


def rms_norm(x, g):
    xf = x.astype(jnp.float32)
    y = xf * lax.rsqrt(jnp.mean(xf * xf, axis=-1, keepdims=True) + EPS)
    return (y * g.astype(jnp.float32)).astype(x.dtype)


def forgetting_attention(q, k, v, f_logit):
    b, s, h, dh = q.shape
    scale = dh ** -0.5
    c = jnp.cumsum(jax.nn.log_sigmoid(f_logit.astype(jnp.float32)), axis=1).transpose(0, 2, 1)
    qh = q.transpose(0, 2, 1, 3)
    kh = k.transpose(0, 2, 1, 3)
    vh = v.transpose(0, 2, 1, 3)
    outs = []
    for blk in range(s // Q_BLOCK):
        q0 = blk * Q_BLOCK
        q1 = q0 + Q_BLOCK
        logits = jnp.einsum('bhqd,bhkd->bhqk', qh[:, :, q0:q1], kh[:, :, :q1]).astype(jnp.float32) * scale
        logits = logits + c[:, :, q0:q1, None] - c[:, :, None, :q1]
        causal = (q0 + jnp.arange(Q_BLOCK))[:, None] >= jnp.arange(q1)[None, :]
        probs = jax.nn.softmax(jnp.where(causal, logits, -jnp.inf), axis=-1)
        outs.append(jnp.einsum('bhqk,bhkd->bhqd', probs.astype(v.dtype), vh[:, :, :q1]))
    o = jnp.concatenate(outs, axis=2)
    return o.transpose(0, 2, 1, 3).reshape(b, s, h * dh)


def hgrn2_recurrence(q, k, v, log_f):
    b, s, h, dk = q.shape
    dv = v.shape[-1]
    n = s // CHUNK

    def to_chunks(t):
        return t.astype(jnp.float32).reshape(b, n, CHUNK, h, t.shape[-1]).transpose(1, 0, 3, 2, 4)

    tri = jnp.arange(CHUNK)[:, None] >= jnp.arange(CHUNK)[None, :]

    def step(state, xs):
        qc, kc, vc, gc = xs
        cum = jnp.cumsum(gc, axis=2)
        o_inter = jnp.einsum('bhtk,bhkv->bhtv', qc * jnp.exp(cum), state)
        rel = jnp.where(tri[:, :, None], cum[:, :, :, None, :] - cum[:, :, None, :, :], -jnp.inf)
        scores = jnp.einsum('bhtk,bhsk,bhtsk->bhts', qc, kc, jnp.exp(rel))
        o_intra = jnp.einsum('bhts,bhsv->bhtv', scores, vc)
        last = cum[:, :, -1:, :]
        new_state = jnp.exp(last[:, :, 0, :])[..., None] * state + jnp.einsum('bhsk,bhsv->bhkv', kc * jnp.exp(last - cum), vc)
        return new_state, o_inter + o_intra

    state0 = jnp.zeros((b, h, dk, dv), jnp.float32)
    _, o = lax.scan(step, state0, (to_chunks(q), to_chunks(k), to_chunks(v), to_chunks(log_f)))
    return o.transpose(1, 0, 3, 2, 4).reshape(b, s, h, dv).astype(q.dtype)


def hgrn_lower_bounds(w_lb):
    cum = jnp.cumsum(jax.nn.softmax(w_lb.astype(jnp.float32), axis=0), axis=0)
    return cum - cum[0:1]


def token_mixer(h, w_in, b_f, lb, g_o, w_out):
    b, s, _ = h.shape
    z = h @ w_in
    fox_shape = (b, s, FOX_HEADS, FOX_HEAD_DIM)
    q_a = z[..., :OFF_FOX_K].reshape(fox_shape)
    k_a = z[..., OFF_FOX_K:OFF_FOX_V].reshape(fox_shape)
    v_a = z[..., OFF_FOX_V:OFF_FOX_F].reshape(fox_shape)
    f_a = z[..., OFF_FOX_F:OFF_HG_Q] + b_f
    y_fox = forgetting_attention(q_a, k_a, v_a, f_a)

    hg_shape = (b, s, HGRN_HEADS, HGRN_EXPAND)
    hv_shape = (b, s, HGRN_HEADS, HGRN_DV)
    q_b = z[..., OFF_HG_Q:OFF_HG_F]
    f_b = z[..., OFF_HG_F:OFF_HG_I].astype(jnp.float32)
    i_b = z[..., OFF_HG_I:OFF_HG_G]
    g_b = z[..., OFF_HG_G:]
    log_f = jnp.logaddexp(jnp.log(lb), jnp.log1p(-lb) + jax.nn.log_sigmoid(f_b))
    k_b = (1.0 - lb) * jax.nn.sigmoid(-f_b)
    o_b = hgrn2_recurrence(q_b.reshape(hg_shape), k_b.reshape(hg_shape), i_b.reshape(hv_shape), log_f.reshape(hg_shape))
    o_b = rms_norm(o_b, g_o) * jax.nn.silu(g_b.reshape(hv_shape))
    y = jnp.concatenate([y_fox, o_b.reshape(b, s, D_HGRN)], axis=-1)
    return y @ w_out


def conv_ffn(h, w_up, conv_w, conv_b, w_down):
    s = h.shape[1]
    u = h @ w_up
    u_pad = jnp.pad(u, ((0, 0), (CONV_WIDTH - 1, 0), (0, 0)))
    u = sum(conv_w[j] * u_pad[:, j:j + s] for j in range(CONV_WIDTH)) + conv_b
    gate, up = jnp.split(u, 2, axis=-1)
    return (jax.nn.silu(gate) * up) @ w_down


def setup_inputs(seed: int = 0) -> dict:
    key = jax.random.key(seed)
    ks = jax.random.split(key, 20)
    f32 = jnp.float32

    def nrm(k, shape, fan_in):
        return jax.random.normal(k, shape, f32) * fan_in ** -0.5

    def gain(k, shape):
        return 1.0 + 0.01 * jax.random.normal(k, shape, f32)

    return {
        'x': jax.random.normal(ks[0], (BATCH, SEQ, D_MODEL), f32),
        'p': jax.random.normal(ks[1], (DEPTH, BATCH, SEQ, PLE_DIM), f32),
        'g_mix_pre': gain(ks[2], (DEPTH, D_MODEL)),
        'w_in': nrm(ks[3], (DEPTH, D_MODEL, N_IN), D_MODEL),
        'b_fox_f': jax.random.uniform(ks[4], (DEPTH, FOX_HEADS), f32, minval=1.0, maxval=4.0),
        'w_hgrn_lb': 0.1 * jax.random.normal(ks[5], (DEPTH, D_HGRN), f32),
        'g_hgrn_out': gain(ks[6], (DEPTH, HGRN_DV)),
        'w_out': nrm(ks[7], (DEPTH, D_MIX, D_MODEL), D_MIX),
        'g_mix_post': gain(ks[8], (DEPTH, D_MODEL)),
        'g_ffn_pre': gain(ks[9], (DEPTH, D_MODEL)),
        'w_up': nrm(ks[10], (DEPTH, D_MODEL, 2 * D_FF), D_MODEL),
        'conv_w': nrm(ks[11], (DEPTH, CONV_WIDTH, 2 * D_FF), CONV_WIDTH),
        'conv_b': 0.01 * jax.random.normal(ks[12], (DEPTH, 2 * D_FF), f32),
        'w_down': nrm(ks[13], (DEPTH, D_FF, D_MODEL), D_FF),
        'g_ffn_post': gain(ks[14], (DEPTH, D_MODEL)),
        'g_ple_in': gain(ks[15], (DEPTH, D_MODEL)),
        'w_ple_gate': nrm(ks[16], (DEPTH, D_MODEL, D_MODEL), D_MODEL),
        'w_ple_proj': nrm(ks[17], (DEPTH, PLE_DIM, D_MODEL), PLE_DIM),
        'g_ple_post': gain(ks[18], (DEPTH, D_MODEL)),
    }


def reference(x, p, g_mix_pre, w_in, b_fox_f, w_hgrn_lb, g_hgrn_out, w_out, g_mix_post,
              g_ffn_pre, w_up, conv_w, conv_b, w_down, g_ffn_post,
              g_ple_in, w_ple_gate, w_ple_proj, g_ple_post):
    lbs = hgrn_lower_bounds(w_hgrn_lb)
    h = x
    for i in range(DEPTH):
        mix = token_mixer(rms_norm(h, g_mix_pre[i]), w_in[i], b_fox_f[i], lbs[i], g_hgrn_out[i], w_out[i])
        h = h + rms_norm(mix, g_mix_post[i])
        ff = conv_ffn(rms_norm(h, g_ffn_pre[i]), w_up[i], conv_w[i], conv_b[i], w_down[i])
        h = h + rms_norm(ff, g_ffn_post[i])
        e = p[i] @ w_ple_proj[i]
        gate = jax.nn.sigmoid(rms_norm(h, g_ple_in[i]) @ w_ple_gate[i])
        h = h + rms_norm(e * gate, g_ple_post[i])
    return h
```

```python
import numpy as np
import concourse.bass as bass
import concourse.mybir as mybir
from concourse.bass_utils import run_bass_kernel_spmd
from contextlib import ExitStack

F32 = mybir.dt.float32
BF16 = mybir.dt.bfloat16
U8 = mybir.dt.uint8
AF = mybir.ActivationFunctionType
ALU = mybir.AluOpType
AX = mybir.AxisListType

DEPTH = 2
D = 2048
NTOK = 1024
NT = 8
KC = 16
SEQ = 4096
N_IN = 7176
D_FF = 5632
NFB = 44
EPS = 1e-6
SCALE = 128 ** -0.5
ENGS = ["pe", "act", "dve", "pool", "sp"]


class Sched:
    def __init__(self, nc):
        self.nc = nc
        self.ops = {e: [] for e in ENGS}
        self.tick = {e: 0 for e in ENGS}
        self.seen = {e: {} for e in ENGS}
        self.lastw = {}
        self.readers = {}
        self.cnt = {}
        self.semnames = set("c_" + e for e in ENGS)
        self.rv = {}

    def op(self, eng, fn, r=(), w=(), dma=None, inc=None):
        own = "c_" + eng
        waits = {}

        def need(sem, val, war):
            if sem == own and dma is None:
                if eng == "pe" or war:
                    return
            if self.seen[eng].get(sem, 0) >= val:
                return
            if waits.get(sem, 0) < val:
                waits[sem] = val

        for k in r:
            for s, v in self.lastw.get(k, {}).items():
                need(s, v, False)
        for k in w:
            for s, v in self.lastw.get(k, {}).items():
                need(s, v, False)
            for s, v in self.readers.get(k, {}).items():
                need(s, v, True)
        for s, v in waits.items():
            self.seen[eng][s] = v
        if dma is not None:
            sem = dma
            step = 16 if inc is None else inc
            self.cnt[sem] = self.cnt.get(sem, 0) + step
            val = self.cnt[sem]
            self.semnames.add(sem)
        else:
            sem = own
            step = 1
            self.tick[eng] += 1
            val = self.tick[eng]
        for k in w:
            self.lastw.setdefault(k, {})[sem] = val
            self.readers[k] = {}
        for k in r:
            self.readers.setdefault(k, {})[sem] = val
        self.ops[eng].append((sorted(waits.items()), fn, sem, step))

    def barrier(self, skip=()):
        allv = {("c_" + e): self.tick[e] for e in ENGS if self.tick[e] > 0}
        allv.update({k_: v_ for k_, v_ in self.cnt.items() if not (skip and k_.startswith(tuple(skip)))})
        for e in ENGS:
            waits = []
            for s, v in allv.items():
                if s == "c_" + e:
                    continue
                if self.seen[e].get(s, 0) < v:
                    waits.append((s, v))
                    self.seen[e][s] = v
            if waits:
                self.ops[e].append((sorted(waits), None, None, 0))

    def mm(self, out, lhsT, rhs, start=True, stop=True, r=(), w=()):
        self.op("pe", lambda e: e.matmul(out, lhsT, rhs, start=start, stop=stop), r, w)

    def tr(self, out, in_, ident, r=(), w=()):
        self.op("pe", lambda e: e.transpose(out, in_, ident), r, w)

    def act(self, out, in_, func, bias=None, scale=None, accum=None, r=(), w=()):
        kw = {}
        if bias is not None:
            kw["bias"] = bias
        if scale is not None:
            kw["scale"] = scale
        if accum is not None:
            kw["accum_out"] = accum
        self.op("act", lambda e: e.activation(out, in_, func, **kw), r, w)

    def ts(self, eng, out, in0, s1, s2, op0, op1=None, r=(), w=()):
        if op1 is None:
            self.op(eng, lambda e: e.tensor_scalar(out, in0, s1, None, op0), r, w)
        else:
            self.op(eng, lambda e: e.tensor_scalar(out, in0, s1, s2, op0, op1), r, w)

    def tt(self, eng, out, in0, in1, op, r=(), w=()):
        self.op(eng, lambda e: e.tensor_tensor(out, in0, in1, op), r, w)

    def stt(self, eng, out, in0, scalar, in1, op0, op1, r=(), w=()):
        self.op(eng, lambda e: e.scalar_tensor_tensor(out, in0, scalar, in1, op0, op1), r, w)

    def copy(self, eng, out, in_, r=(), w=()):
        if eng == "act":
            self.op(eng, lambda e: e.activation(out, in_, AF.Identity), r, w)
        else:
            self.op(eng, lambda e: e.tensor_copy(out, in_), r, w)

    def memset(self, eng, ap, val, w=()):
        self.op(eng, lambda e: e.memset(ap, val), (), w)

    def dma(self, eng, out, in_, sem, r=(), w=(), slow=False):
        def fn(e):
            rvx = RV(self.rv[eng], e) if (callable(out) or callable(in_)) else None
            o = out(rvx) if callable(out) else out
            i = in_(rvx) if callable(in_) else in_
            try:
                if slow:
                    return e.dma_start(out=o, in_=i, allow_slow_non_contiguous=True)
                return e.dma_start(out=o, in_=i)
            except Exception:
                print("DMA FAIL", sem, o.shape, o.ap, i.shape, i.ap, flush=True)
                raise
        self.op(eng, fn, r, w, dma="d_" + sem)

    def replay(self, stack):
        nc = self.nc
        sems = {}
        for n in sorted(self.semnames):
            sems[n] = stack.enter_context(nc.semaphore(n))
        block = stack.enter_context(nc.Block())

        def mk(en):
            def body(e):
                if en in ("sp", "pool"):
                    r_ = e.snap(e.partition_id() % 4)
                    self.rv[en] = (r_, e.snap(r_ // 2), e.snap(r_ % 2))
                for waits, fn, sem, step in self.ops[en]:
                    for s, v in waits:
                        e.wait_ge(sems[s], v)
                    if fn is not None:
                        fn(e).then_inc(sems[sem], step)
            return body

        block.tensor(mk("pe"))
        block.scalar(mk("act"))
        block.vector(mk("dve"))
        block.gpsimd(mk("pool"))
        block.sync(mk("sp"))


class RV:
    def __init__(self, vals, e):
        self.vals = vals
        self.e = e

    def __getitem__(self, i):
        return self.vals[i]

    def sn(self, expr):
        return self.e.snap(expr, donate=True)


class Arena:
    def __init__(self, nc, nbytes):
        self.t = nc.alloc_sbuf_tensor("arena", [128, nbytes], U8)
        self.nbytes = nbytes
        self.off = 0

    def mark(self):
        return self.off

    def reset(self, m):
        self.off = m

    def alloc(self, shape, dtype, npart=128):
        esz = 4 if dtype == F32 else 2
        n = 1
        for s in shape:
            n *= s
        nb = (n * esz + 31) // 32 * 32
        assert self.off + nb <= self.nbytes, (self.off, nb, self.nbytes)
        ap = self.t[0:npart, self.off:self.off + nb // 1]
        ap = self.t[0:npart, self.off:self.off + n * esz].bitcast(dtype)
        self.off += nb
        if len(shape) == 2:
            ap = ap.rearrange("p (a b) -> p a b", b=shape[1])
        elif len(shape) == 3:
            ap = ap.rearrange("p (a b c) -> p a b c", b=shape[1], c=shape[2])
        return ap


def build_program(debug=False, upto=99):
    nc = bass.Bass("TRN2", target_bir_lowering=False)
    S = Sched(nc)
    dt_in = {}

    def inp(name, shape):
        dt_in[name] = nc.dram_tensor(name, shape, F32, kind="ExternalInput")
        return dt_in[name]

    x = inp("x", [NTOK, D])
    p_in = inp("p", [DEPTH, NTOK, 256])
    g_mix_pre = inp("g_mix_pre", [DEPTH, D])
    w_in = inp("w_in", [DEPTH, D, N_IN])
    b_fox_f = inp("b_fox_f", [DEPTH, 8])
    w_hgrn_lb = inp("w_hgrn_lb", [DEPTH, 1024])
    g_hgrn_out = inp("g_hgrn_out", [DEPTH, 128])
    w_out = inp("w_out", [DEPTH, D, D])
    g_mix_post = inp("g_mix_post", [DEPTH, D])
    g_ffn_pre = inp("g_ffn_pre", [DEPTH, D])
    w_up = inp("w_up", [DEPTH, D, 2 * D_FF])
    conv_w = inp("conv_w", [DEPTH, 3, 2 * D_FF])
    conv_b = inp("conv_b", [DEPTH, 2 * D_FF])
    w_down = inp("w_down", [DEPTH, D_FF, D])
    g_ffn_post = inp("g_ffn_post", [DEPTH, D])
    g_ple_in = inp("g_ple_in", [DEPTH, D])
    w_ple_gate = inp("w_ple_gate", [DEPTH, D, D])
    w_ple_proj = inp("w_ple_proj", [DEPTH, 256, D])
    g_ple_post = inp("g_ple_post", [DEPTH, D])
    c_ident = inp("c_ident", [128, 128])
    c_maskT = inp("c_maskT", [128, 128])
    c_ones = inp("c_ones", [128, 128])
    c_striu = inp("c_striu", [32, 32])
    c_seg = inp("c_seg", [128, 1024])
    c_hmask = inp("c_hmask", [128, 1])
    c_sel = inp("c_sel", [128, 4])
    out = nc.dram_tensor("out", [NTOK, D], F32, kind="ExternalOutput")

    hbuf = nc.dram_tensor("hbuf", [NTOK, D], F32)
    hn2buf = nc.dram_tensor("hn2buf", [NTOK, D], BF16)
    ZS = [nc.dram_tensor("ZS", [20 * 512, 1024], BF16)] * DEPTH
    GZ = [nc.dram_tensor("GZ", [20 * 2048, 1024], BF16)] * DEPTH
    ysA = [nc.dram_tensor(f"ysA{j}", [SEQ, 128], BF16) for j in range(2)]
    OH = nc.dram_tensor("OH", [8 * 1024, 128], F32)
    ST = nc.dram_tensor("ST", [1032, 128], F32)
    GST = nc.dram_tensor("GST", [4 * 1032, 128], F32)
    GYA = [nc.dram_tensor(f"GYA{j}", [4 * 81920, 128], BF16) for j in range(2)]
    hh = [nc.dram_tensor("hh", [2, D], F32)] * DEPTH
    ghh = [nc.dram_tensor("ghh", [8, D], F32)] * DEPTH
    GROUPS = [[0, 1, 2, 3], [4, 5, 6, 7]]
    import os
    NOAG = bool(os.environ.get("K_NOAG"))
    agn = [0]

    def allgather(sa, da, rkey, wkey, semname="cc_z"):
        if NOAG:
            return
        agn[0] += 1
        S.op("pool", lambda e: e.collective_compute("AllGather", ALU.bypass, replica_groups=GROUPS,
                                                    ins=[sa], outs=[da]),
             r=list(rkey) if isinstance(rkey, (list, tuple)) else [rkey], w=[wkey], dma=semname, inc=1)

    dbg = {}

    def dbg_out(name, src):
        if debug:
            shape = list(src.shape)
            t = nc.dram_tensor("dbg_" + name, shape, src.dtype, kind="ExternalOutput")
            dbg[name] = (t, src)

    A = Arena(nc, 207 * 1024)
    psb = [nc.alloc_psum_tensor(f"ps{i}", [128, 512], F32) for i in range(8)]

    def ps32(b):
        return psb[b][:, :]

    def ps16(b):
        return psb[b][:, :].bitcast(BF16)

    ident = A.alloc([128], BF16)
    maskT = A.alloc([128], BF16)
    tri = A.alloc([128], F32)
    ones = A.alloc([128], F32)
    striu = A.alloc([32], F32)
    mask64 = A.alloc([64], F32)
    seg = A.alloc([1024], F32)
    onec = A.alloc([1], F32)
    epsc = A.alloc([1], F32)
    hmask = A.alloc([1], F32)
    selc = A.alloc([4], F32)
    gb = [A.alloc([D], F32), A.alloc([D], F32)]
    cw_all = [A.alloc([3, 88], F32) for _ in range(DEPTH)]
    cb_all = [A.alloc([88], F32) for _ in range(DEPTH)]
    S.dma("pool", ident, c_ident[:, :], "ident", w=["ident"])
    S.dma("pool", maskT, c_maskT[:, :], "maskT", w=["maskT"])
    S.dma("sp", tri, c_maskT[:, :], "tri", w=["tri"])
    S.dma("sp", ones, c_ones[:, :], "ones", w=["ones"])
    S.dma("sp", striu[0:32, :], c_striu[:, :], "striu", w=["striu"])
    S.dma("sp", mask64[0:64, :], c_maskT[0:64, 0:64], "mask64", w=["mask64"])
    S.dma("sp", seg, c_seg[:, :], "seg", w=["seg"])
    S.dma("sp", hmask, c_hmask[:, :], "hmask", w=["hmask"])
    S.dma("sp", selc, c_sel[:, :], "selc", w=["selc"])
    identf = A.alloc([128], F32)
    cwT = A.alloc([4, 128], F32)
    S.dma("sp", identf, c_ident[:, :], "identf", w=["identf"])
    for l_ in range(DEPTH):
        S.dma("sp", cwT[0:88, 0:3, :], conv_w[l_].rearrange("j (fb p) -> fb j p", p=128), "cwT", w=["cwT"])
        S.dma("sp", cwT[0:88, 3, :], conv_b[l_].rearrange("(fb p) -> fb p", p=128), "cwT", w=["cwT"])
        for j4 in range(4):
            S.mm(ps32(4)[:, j4 * 88:(j4 + 1) * 88], cwT[0:88, j4, :], identf[0:88, 0:88], r=["cwT", "identf"], w=["ps4"])
        S.copy("dve", cw_all[l_], ps32(4)[:, 0:264].rearrange("p (j f) -> p j f", f=88), r=["ps4"], w=["cw"])
        S.copy("dve", cb_all[l_], ps32(4)[:, 264:352], r=["ps4"], w=["cb"])
    S.memset("dve", onec, 1.0, w=["onec"])
    S.memset("dve", epsc, EPS, w=["epsc"])
    gbi = [0]

    def load_gain(gt, l, n=D, npart=128):
        i = gbi[0] % 2
        gbi[0] += 1
        key = f"gb{i}"
        S.dma("sp", gb[i][0:npart, 0:n], gt[l, :].partition_broadcast(npart), key, w=[key])
        return gb[i], key

    rr = [0]

    def evac_eng():
        rr[0] += 1
        return "act" if rr[0] % 2 else "dve"

    base_mark = A.mark()

    def norm_transpose(src_ap, src_keys, npart, gain, gkey, dstT, dst_key, col0, bufs, pre_scale=None, part=0):
        junk, ss, sd, rstd, hnb, tag = bufs
        if part in (0, 1):
            _norm_stats(src_ap, src_keys, npart, gain, gkey, bufs)
        if part in (0, 2):
            _norm_tr(npart, dstT, dst_key, col0, bufs)

    def _norm_stats(src_ap, src_keys, npart, gain, gkey, bufs):
        junk, ss, sd, rstd, hnb, tag = bufs
        S.act(junk[0:npart, :], src_ap, AF.Square, accum=ss[0:npart, :], r=src_keys, w=[tag + "junk", tag + "ss"])
        S.act(sd[0:npart, :], ss[0:npart, :], AF.Sqrt, bias=epsc[0:npart, :], scale=1.0 / D,
              r=[tag + "ss", "epsc"], w=[tag + "sd"])
        S.op("dve", lambda e: e.reciprocal(rstd[0:npart, :], sd[0:npart, :]), [tag + "sd"], [tag + "rstd"])
        S.stt("dve", hnb[0:npart, :], src_ap, rstd[0:npart, :], gain[0:npart, :], ALU.mult, ALU.mult,
              r=src_keys + [tag + "rstd", gkey], w=[tag + "hnb"])

    def _norm_tr(npart, dstT, dst_key, col0, bufs):
        junk, ss, sd, rstd, hnb, tag = bufs
        for half in range(2):
            bank = 6 + half
            pv = ps16(bank)
            for k8 in range(8):
                kc = half * 8 + k8
                S.tr(pv[:, k8 * 128:k8 * 128 + npart], hnb[0:npart, kc * 128:(kc + 1) * 128], ident[0:npart, 0:npart],
                     r=[tag + "hnb", "ident"], w=[f"ps{bank}"] if k8 in (0, 7) else [])
            src = pv.rearrange("p (a b) -> p a b", b=128)[:, :, 0:npart]
            S.copy(evac_eng(), dstT[:, half * 8:(half + 1) * 8, col0:col0 + npart], src,
                   r=[f"ps{bank}"], w=[dst_key])

    def resid_update(val_ap, val_keys, gain, gkey, hsrc_ap, hdst_aps, bufs, tagkey, alias=()):
        junk, ss, sd, rstd, ht, tag = bufs
        S.dma("sp", ht, hsrc_ap, tag + "ht", r=[tagkey], w=[tag + "ht"] + list(alias))
        S.act(junk, val_ap, AF.Square, accum=ss, r=val_keys, w=[tag + "junk", tag + "ss"])
        S.act(sd, ss, AF.Sqrt, bias=epsc, scale=1.0 / D, r=[tag + "ss", "epsc"], w=[tag + "sd"])
        S.op("dve", lambda e: e.reciprocal(rstd, sd), [tag + "sd"], [tag + "rstd"])
        S.stt("dve", val_ap, val_ap, rstd, gain, ALU.mult, ALU.mult, r=val_keys + [tag + "rstd", gkey], w=val_keys)
        S.tt("dve", ht, ht, val_ap, ALU.add, r=val_keys + [tag + "ht"], w=[tag + "ht"])
        for hd in hdst_aps:
            S.dma("sp", hd, ht, tag + "ht", r=[tag + "ht"], w=[tagkey])

    for l in range(DEPTH):
        hsrc = x if l == 0 else hbuf
        A.reset(base_mark)
        hnT = A.alloc([KC, NTOK], BF16)
        wsl = [A.alloc([KC, 512], BF16) for _ in range(2)]
        ht = [A.alloc([D], F32) for _ in range(2)]
        hnb = [A.alloc([D], BF16) for _ in range(2)]
        junk = A.alloc([D], BF16)
        st_ss = [A.alloc([1], F32) for _ in range(6)]
        stg_fm = [A.alloc([1024], F32) for _ in range(2)]
        stg_tm = [A.alloc([8, 512], BF16) for _ in range(2)]
        stg_fa = A.alloc([8, 8], F32)
        wsm = A.alloc([KC, 8], BF16)
        gain, gkey = load_gain(g_mix_pre, l)
        hk_all = [f"hnT{t_}" for t_ in range(NT)]

        def p1_stats(i, hsrc=hsrc, gain=gain, gkey=gkey):
            b = i % 2
            S.dma("sp", ht[b], hsrc[i * 128:(i + 1) * 128, :], f"p1ht{b}", r=[f"hbuf{i}"], w=[f"p1ht{b}"])
            norm_transpose(ht[b], [f"p1ht{b}"], 128, gain, gkey, hnT, f"hnT{i}", i * 128,
                           (junk, st_ss[0 + b], st_ss[2 + b], st_ss[4 + b], hnb[b], f"p1n{b}"), part=1)

        def p1_tr(i, gain=gain, gkey=gkey):
            b = i % 2
            norm_transpose(ht[b], [f"p1ht{b}"], 128, gain, gkey, hnT, f"hnT{i}", i * 128,
                           (junk, st_ss[0 + b], st_ss[2 + b], st_ss[4 + b], hnb[b], f"p1n{b}"), part=2)

        p1_stats(0)
        p1_stats(1)
        p1_tr(0)
        if upto < 1:
            for i in range(NT):
                if i + 2 < NT:
                    p1_stats(i + 2)
                if i + 1 < NT:
                    p1_tr(i + 1)
        if upto >= 1:
            w_l = w_in[l].rearrange("(kc p) n -> p kc n", p=128)
            blocks = [("tm", 5128, 3, 0, 0), ("fm", 3080, 1, 2, 0), ("fm", 3592, 1, 2, 4),
                      ("ff", 4104, 2, 0, 0), ("ff", 4616, 2, 0, 4),
                      ("tm", 5640, 3, 0, 4), ("tm", 6152, 3, 2, 0), ("tm", 6664, 3, 2, 4),
                      ("fm", 0, 0, 0, 0), ("fm", 512, 0, 0, 4), ("fm", 1024, 0, 2, 0), ("fm", 1536, 0, 2, 4),
                      ("tm", 2048, 1, 0, 0), ("tm", 2560, 1, 0, 4), ("fa", 3072, 4, 0, 0)]
            done_at = {11: 0, 13: 1, 14: 4}
            bi = 0
            sfm = 0
            stm = 0
            pb = 0
            pending = []
            agq = []

            def flush(upto_n):
                while pending and pending[0][0] <= upto_n:
                    pending.pop(0)[1]()

            zs_t = ZS[l].ap().rearrange("(ci s x) (y c) -> ci s (x y) c", ci=20, s=4, c=128).rearrange(
                "ci s (i p) c -> p ci s i c", p=128)
            zs_f = ZS[l].ap().bitcast(F32).rearrange("(ci h p two) c -> p ci h (two c)", ci=20, h=2, two=2)
            zs_fa = ZS[l].ap().bitcast(F32).rearrange("(ci x) c -> ci (x c)", ci=20)
            for n_blk, (kind, c0, kch, slot0, h0) in enumerate(blocks):
                sl = bi % 2
                bi += 1
                wc0 = c0 + 8 - 512 if kind == "fa" else c0
                S.dma("pool", wsl[sl], w_l[:, :, wc0:wc0 + 512], f"wsl{sl}", w=[f"wsl{sl}"])
                flush(n_blk - 2)
                if kind == "fa":
                    for i in range(NT):
                        bank = pb % 4
                        pb += 1
                        for kc in range(KC):
                            S.mm(ps32(bank)[:, 0:8], hnT[:, kc, i * 128:(i + 1) * 128], wsl[sl][:, kc, 504:512],
                                 start=(kc == 0), stop=(kc == KC - 1), r=[f"hnT{i}", f"wsl{sl}"],
                                 w=[f"ps{bank}"] if kc in (0, KC - 1) else [])
                        S.copy(evac_eng(), stg_fa[:, i, :], ps32(bank)[:, 0:8], r=[f"ps{bank}"], w=["stg_fa"])
                    for rp in range(4):
                        S.dma("sp", zs_fa[rp * 5 + 4, 0:2048].rearrange("(i p j) -> p i j", p=128, j=2),
                              stg_fa[:, :, 2 * rp:2 * rp + 2], "stg_fa", r=["stg_fa"], w=["ZS"])
                elif kind in ("fm", "ff"):
                    for fb in range(4):
                        h = h0 + fb
                        ci = (h // 2) * 5 + kch
                        jj = h % 2
                        sb = sfm % 2
                        sfm += 1
                        stg = stg_fm[sb] if kind == "ff" else stg_fm[sb].bitcast(BF16)[:, 0:1024]
                        for half in range(2):
                            bank = pb % 4
                            pb += 1
                            for kc in range(KC):
                                S.mm(ps32(bank), wsl[sl][:, kc, fb * 128:(fb + 1) * 128],
                                     hnT[:, kc, half * 512:(half + 1) * 512],
                                     start=(kc == 0), stop=(kc == KC - 1), r=hk_all + [f"wsl{sl}"],
                                     w=[f"ps{bank}"] if kc in (0, KC - 1) else [])
                            S.copy(evac_eng(), stg[:, half * 512:(half + 1) * 512], ps32(bank),
                                   r=[f"ps{bank}"], w=[f"stg_fm{sb}"])
                        if kind == "ff":
                            dz = zs_f[:, ci, jj, :]
                        else:
                            r0 = ci * 512 + (slot0 + jj) * 128
                            dz = ZS[l][r0:r0 + 128, :]
                        S.dma("sp", dz, stg, f"stg_fm{sb}", r=[f"stg_fm{sb}"], w=["ZS"])
                else:
                    sb = stm % 2
                    stm += 1
                    for i in range(NT):
                        bank = pb % 4
                        pb += 1
                        if n_blk == 0 and i + 2 < NT:
                            p1_stats(i + 2)
                        for kc in range(KC):
                            S.mm(ps32(bank), hnT[:, kc, i * 128:(i + 1) * 128], wsl[sl][:, kc, :],
                                 start=(kc == 0), stop=(kc == KC - 1), r=[f"hnT{i}", f"wsl{sl}"],
                                 w=[f"ps{bank}"] if kc in (0, KC - 1) else [])
                        S.copy(evac_eng(), stg_tm[sb][:, i, :], ps32(bank), r=[f"ps{bank}"], w=[f"stg_tm{sb}"])
                        if n_blk == 0 and i + 1 < NT:
                            p1_tr(i + 1)
                    for w4 in range(4):
                        h = h0 + w4
                        ci = (h // 2) * 5 + kch
                        S.dma("sp", zs_t[:, ci, slot0 + h % 2, :, :], stg_tm[sb][:, :, w4 * 128:(w4 + 1) * 128],
                              f"stg_tm{sb}", r=[f"stg_tm{sb}"], w=["ZS"])
                if n_blk in done_at:
                    kd = done_at[n_blk]
                    for rp in range(4):
                        ci = rp * 5 + kd
                        nrow = {0: 512, 1: 256, 4: 4}[kd]
                        sa_ = ZS[l][ci * 512:ci * 512 + nrow, :]
                        da_ = GZ[l][ci * 2048:ci * 2048 + 4 * nrow, :]
                        if kd == 4:
                            sa_, da_ = sa_.bitcast(F32), da_.bitcast(F32)
                        agq.append((kd, (lambda sa__, da__, kd_: (lambda extra: allgather(
                            sa__, da__, ["ZS"] + extra, f"GZk{kd_}", f"cc_z{kd_}")))(sa_, da_, kd)))
        S.barrier(skip=("cc_",))
        if upto < 1:
            break
        A.reset(base_mark)
        yT = A.alloc([KC, NTOK], BF16)
        yT_mark = A.mark()
        wo = A.alloc([KC, D], BF16)
        wo_mark = A.mark()
        A.reset(yT_mark)
        qtl_all = A.alloc([8, 1024], BF16)
        e2_all = A.alloc([8, 16], F32)
        Sloc = A.alloc([8, 128], F32)
        dtot = A.alloc([8], F32)
        o_all = A.alloc([8, 16, 128], F32)
        oss = A.alloc([16], F32)
        osd = A.alloc([16], F32)
        ors = A.alloc([16], F32)
        emid = A.alloc([16], F32)
        dlast = A.alloc([16], F32)
        dlm = A.alloc([16], F32)
        lsum = A.alloc([16], F32)
        pinc = A.alloc([16], F32)
        ones16 = A.alloc([16], F32)
        Sq = [A.alloc([128], BF16) for _ in range(4)]
        lbw8 = A.alloc([2, 8], F32)
        lbv = A.alloc([8], F32)
        oml = A.alloc([8], F32)
        gho = A.alloc([128], F32)
        hg_mark = A.mark()
        qTb2 = [A.alloc([1024], BF16) for _ in range(3)]
        fTb2 = [A.alloc([1024], F32) for _ in range(3)]
        vch2 = [A.alloc([16, 128], BF16) for _ in range(3)]
        t1 = A.alloc([1024], F32)
        t2 = A.alloc([1024], BF16)
        t3 = A.alloc([1024], F32)
        t4 = A.alloc([1024], F32)
        ktl = A.alloc([1024], BF16)
        kt = A.alloc([16, 128], BF16)
        scT = A.alloc([16, 64], BF16)
        dS_all = A.alloc([16, 128], F32)
        S.memset("dve", ones16, 1.0, w=["ones16"])
        S.dma("sp", gho[0:64, :], g_hgrn_out[l, :].partition_broadcast(64), "gho", w=["gho"])
        if l == 0:
            S.memset("dve", lbv, 0.0, w=["lbv"])
        else:
            for li in range(2):
                S.dma("sp", lbw8[:, li, :], w_hgrn_lb[li, :].rearrange("(h p) -> p h", p=128), "lbw", w=["lbw"], slow=True)
            S.tt("dve", lbv, lbw8[:, 1, :], lbw8[:, 0, :], ALU.subtract, r=["lbw"], w=["lbv"])
            S.act(lbv, lbv, AF.Sigmoid, r=["lbv"], w=["lbv"])
        S.ts("dve", oml, lbv, -1.0, 1.0, ALU.mult, ALU.add, r=["lbv"], w=["oml"])
        S.memset("dve", Sloc, 0.0, w=["Sloc"])
        zs_n = ZS[l].ap().rearrange("(ci s x) (y c) -> ci s (x y) c", ci=20, s=4, c=128).rearrange(
            "ci s (n p) c -> p ci s n c", p=64)
        zs_ff = ZS[l].ap().bitcast(F32).rearrange("(ci h p two) c -> p ci h (two c)", ci=20, h=2, two=2)
        oh_v = OH.ap().rearrange("(h n p) c -> p h n c", h=8, p=64)
        t1v = t1.rearrange("p (a b) -> p a b", b=64)
        t3v = t3.rearrange("p (a b) -> p a b", b=64)
        def hg_loads(h):
            rp, j = h // 2, h % 2
            b2 = h % 3
            r0 = (rp * 5 + 1) * 512 + (2 + j) * 128
            S.dma("sp", qTb2[b2], ZS[l][r0:r0 + 128, :], f"qTb{b2}", r=["ZS"], w=[f"qTb{b2}"])
            S.dma("sp", fTb2[b2], zs_ff[:, rp * 5 + 2, j, :], f"fTb{b2}", r=["ZS"], w=[f"fTb{b2}"])
            S.dma("sp", vch2[b2][0:64], zs_n[:, rp * 5 + 3, j, :, :], f"vch{b2}", r=["ZS"], w=[f"vch{b2}"])

        agq.sort(key=lambda t_: {4: 0, 0: 1, 1: 2}[t_[0]])
        ag_plan = {2: 5, 3: 1, 4: 1, 5: 1}
        hg_loads(0)
        hg_loads(1)
        for h in range(8):
            rp, j = h // 2, h % 2
            b2 = h % 3
            if h + 2 < 8:
                hg_loads(h + 2)
                nb2 = (h + 2) % 3
                for _ in range(ag_plan.get(h + 2, 0)):
                    if agq:
                        agq.pop(0)[1]([f"qTb{nb2}", f"fTb{nb2}", f"vch{nb2}"])
            qTb, fTb, vch = qTb2[b2], fTb2[b2], vch2[b2]
            kq, kf, kv = f"qTb{b2}", f"fTb{b2}", f"vch{b2}"
            qtl = qtl_all[:, h, :]
            S.act(t1, fTb, AF.Sigmoid, r=[kf], w=["t1"])
            S.ts("dve", t1, t1, oml[:, h:h + 1], lbv[:, h:h + 1], ALU.mult, ALU.add, r=["t1", "oml", "lbv"], w=["t1"])
            S.ts("dve", t2, t1, -1.0, 1.0, ALU.mult, ALU.add, r=["t1"], w=["t2"])
            S.act(t1, t1, AF.Ln, r=["t1"], w=["t1"])
            S.op("dve", lambda e: e.tensor_tensor_scan(t3, seg, t1, 0.0, ALU.mult, ALU.add), ["seg", "t1"], ["t3"])
            S.tt("dve", t1v, t3v, t3v[:, :, 31:32].broadcast_to([128, 16, 64]), ALU.subtract, r=["t3"], w=["t1"])
            S.act(t4, t1, AF.Exp, r=["t1"], w=["t4"])
            S.act(t1, t1, AF.Exp, scale=-1.0, r=["t1"], w=["t1"])
            S.tt("dve", qtl, qTb, t4, ALU.mult, r=[kq, "t4"], w=["qtl_all"])
            S.tt("dve", ktl, t2, t1, ALU.mult, r=["t2", "t1"], w=["ktl"])
            S.act(emid, t3v[:, :, 31], AF.Exp, r=["t3"], w=["emid"])
            S.act(dlast, t3v[:, :, 63], AF.Exp, r=["t3"], w=["dlast"])
            S.tt("dve", dlm, t3v[:, :, 63], t3v[:, :, 31], ALU.subtract, r=["t3"], w=["dlm"])
            S.act(dlm, dlm, AF.Exp, r=["dlm"], w=["dlm"])
            S.copy("dve", lsum, t3v[:, :, 63], r=["t3"], w=["lsum"])
            S.op("dve", lambda e: e.tensor_tensor_scan(pinc, ones16, lsum, 0.0, ALU.mult, ALU.add), ["ones16", "lsum"], ["pinc"])
            S.act(dtot[:, h:h + 1], pinc[:, 15:16], AF.Exp, r=["pinc"], w=["dtot"])
            S.tt("dve", lsum, pinc, lsum, ALU.subtract, r=["pinc", "lsum"], w=["lsum"])
            S.tt("dve", lsum, lsum, t3v[:, :, 31], ALU.add, r=["lsum", "t3"], w=["lsum"])
            S.act(e2_all[:, h, :], lsum, AF.Exp, r=["lsum"], w=["e2_all"])
            for half in range(2):
                bank = 6 + half
                pv = ps16(bank)
                for n8 in range(8):
                    n = half * 8 + n8
                    S.tr(pv[0:64, n8 * 128:(n8 + 1) * 128], ktl[:, n * 64:(n + 1) * 64], ident,
                         r=["ktl", "ident"], w=[f"ps{bank}"] if n8 in (0, 7) else [])
                S.copy(evac_eng(), kt[0:64, half * 8:(half + 1) * 8, :],
                       pv[0:64, :].rearrange("p (a b) -> p a b", b=128), r=[f"ps{bank}"], w=["kt"])
            for half in range(2):
                bank = 4 + half
                for n8 in range(8):
                    n = half * 8 + n8
                    S.mm(ps32(bank)[0:64, n8 * 64:(n8 + 1) * 64], ktl[:, n * 64:(n + 1) * 64],
                         qtl[:, n * 64:(n + 1) * 64], r=["ktl", "qtl_all"], w=[f"ps{bank}"] if n8 in (0, 7) else [])
                S.tt("dve", scT[0:64, half * 8:(half + 1) * 8, :],
                     ps32(bank)[0:64, :].rearrange("p (a b) -> p a b", b=64),
                     mask64[0:64, :].rearrange("p (o b) -> p o b", o=1).broadcast_to([64, 8, 64]), ALU.mult,
                     r=[f"ps{bank}", "mask64"], w=["scT"])
            osb = o_all[:, h, :, :]
            okey = "o_all"
            Sh = Sloc[:, h, :]
            for g4 in range(4):
                bank = g4 % 2 + 2
                for n4 in range(4):
                    n = g4 * 4 + n4
                    S.mm(ps32(bank)[:, n4 * 128:(n4 + 1) * 128], kt[0:64, n, :], vch[0:64, n, :], r=["kt", kv],
                         w=[f"ps{bank}"] if n4 in (0, 3) else [])
                S.tt("dve", dS_all[:, g4 * 4:(g4 + 1) * 4, :], ps32(bank).rearrange("p (a b) -> p a b", b=128),
                     dlm[:, g4 * 4:(g4 + 1) * 4].rearrange("p (a o) -> p a o", o=1).broadcast_to([128, 4, 128]), ALU.mult,
                     r=[f"ps{bank}", "dlm"], w=["dS_all"])
            for n in range(16):
                sq = Sq[n % 4]
                ob = n % 2
                S.ts("dve", sq, Sh, emid[:, n:n + 1], None, ALU.mult, r=["Sloc", "emid"], w=[f"Sq{n % 4}"])
                S.stt("dve", Sh, Sh, dlast[:, n:n + 1], dS_all[:, n, :], ALU.mult, ALU.add,
                      r=["Sloc", "dlast", "dS_all"], w=["Sloc"])
                S.mm(ps32(ob)[0:64, 0:128], qtl[:, n * 64:(n + 1) * 64], sq, start=True, stop=False,
                     r=["qtl_all", f"Sq{n % 4}"], w=[f"ps{ob}"])
                S.mm(ps32(ob)[0:64, 0:128], scT[0:64, n, :], vch[0:64, n, :], start=False, stop=True,
                     r=["scT", kv], w=[f"ps{ob}"])
                S.copy("act", osb[0:64, n, :], ps32(ob)[0:64, 0:128], r=[f"ps{ob}"], w=[okey])
        S.dma("sp", ST[0:1024, :].rearrange("(h k) v -> k h v", h=8), Sloc, "Sloc", r=["Sloc"], w=["ST"])
        S.dma("sp", ST[1024:1032, :].rearrange("h k -> k h"), dtot, "dtot", r=["dtot"], w=["ST"], slow=True)
        allgather(ST.ap(), GST.ap(), "ST", "GST", "cc_s")
        while agq:
            agq.pop(0)[1]([])
        S.barrier(skip=("cc_",))
        A.reset(hg_mark)
        gch2 = [A.alloc([16, 128], BF16) for _ in range(2)]
        osq = A.alloc([16, 128], F32)
        sgl = A.alloc([16, 128], F32)
        ybf = A.alloc([16, 128], BF16)
        gst = A.alloc([4, 8, 128], F32)
        gdt = A.alloc([4, 8], F32)
        Acc = A.alloc([8, 128], F32)
        Sin = A.alloc([8, 128], F32)
        Sinb = A.alloc([8, 128], BF16)
        q2 = [A.alloc([1024], BF16) for _ in range(2)]
        for h in range(2):
            S.dma("sp", gch2[h % 2][0:64], zs_n[:, (h // 2) * 5 + 3, 2 + h % 2, :, :], f"gch{h % 2}", r=["ZS"], w=[f"gch{h % 2}"])
        gst_v = GST.ap().rearrange("(rho x) v -> rho x v", rho=4)
        for rho in range(4):
            S.dma("sp", gst[:, rho, :, :], gst_v[rho, 0:1024, :].rearrange("(h k) v -> k h v", h=8), "gst", r=["GST"], w=["gst"])
            S.dma("sp", gdt[:, rho, :], gst_v[rho, 1024:1032, :].rearrange("h k -> k h"), "gdt", r=["GST"], w=["gdt"], slow=True)
        S.memset("dve", Acc, 0.0, w=["Acc"])
        S.memset("dve", Sin, 0.0, w=["Sin"])
        for rho in range(4):
            S.stt("dve", Sin, Acc, selc[:, rho:rho + 1], Sin, ALU.mult, ALU.add, r=["Acc", "selc", "Sin"], w=["Sin"])
            if rho < 3:
                S.tt("dve", Acc, Acc, gdt[:, rho, :].rearrange("p (h o) -> p h o", o=1).broadcast_to([128, 8, 128]), ALU.mult,
                     r=["Acc", "gdt"], w=["Acc"])
                S.tt("dve", Acc, Acc, gst[:, rho, :, :], ALU.add, r=["Acc", "gst"], w=["Acc"])
        S.copy("dve", Sinb, Sin, r=["Sin"], w=["Sinb"])
        def p2_stage_a(h):
            rp, j = h // 2, h % 2
            osb = o_all[:, h, :, :]
            okey = f"o_all{h}"
            qq = q2[h % 2]
            S.tt("dve", qq.rearrange("p (n t) -> p n t", t=64), qtl_all[:, h, :].rearrange("p (n t) -> p n t", t=64),
                 e2_all[:, h, :].rearrange("p (n o) -> p n o", o=1).broadcast_to([128, 16, 64]), ALU.mult,
                 r=["qtl_all", "e2_all"], w=[f"q2{h % 2}"])
            for g4 in range(4):
                for n4 in range(4):
                    n = g4 * 4 + n4
                    S.mm(ps32(g4)[0:64, n4 * 128:(n4 + 1) * 128], qq[:, n * 64:(n + 1) * 64], Sinb[:, h, :],
                         r=[f"q2{h % 2}", "Sinb"], w=[f"ps{g4}"] if n4 in (0, 3) else [])
                S.tt("dve", osb[0:64, g4 * 4:(g4 + 1) * 4, :], osb[0:64, g4 * 4:(g4 + 1) * 4, :],
                     ps32(g4)[0:64, :].rearrange("p (a b) -> p a b", b=128), ALU.add, r=[okey, f"ps{g4}"], w=[okey])

        def p2_stage_b(h):
            rp, j = h // 2, h % 2
            osb = o_all[:, h, :, :]
            okey = f"o_all{h}"
            gch = gch2[h % 2]
            S.tt("pool", osq[0:64], osb[0:64], osb[0:64], ALU.mult, r=[okey], w=["osq"])
            S.op("dve", lambda e: e.tensor_reduce(oss[0:64, :], osq[0:64], AX.X, ALU.add), ["osq"], ["oss"])
            S.act(osd[0:64, :], oss[0:64, :], AF.Sqrt, bias=epsc[0:64, :], scale=1.0 / 128, r=["oss", "epsc"], w=["osd"])
            S.op("dve", lambda e: e.reciprocal(ors[0:64, :], osd[0:64, :]), ["osd"], ["ors"])
            S.act(sgl[0:64], gch[0:64], AF.Silu, r=[f"gch{h % 2}"], w=["sgl"])
            S.tt("pool", sgl[0:64], sgl[0:64],
                 gho[0:64, :].rearrange("p (o b) -> p o b", o=1).broadcast_to([64, 16, 128]), ALU.mult,
                 r=["sgl", "gho"], w=["sgl"])
            S.tt("dve", osq[0:64], osb[0:64],
                 ors[0:64, :].rearrange("p (a o) -> p a o", o=1).broadcast_to([64, 16, 128]), ALU.mult,
                 r=[okey, "ors"], w=["osq"])
            if h + 2 < 8:
                h2 = h + 2
                S.dma("sp", gch2[h2 % 2][0:64], zs_n[:, (h2 // 2) * 5 + 3, 2 + h2 % 2, :, :], f"gch{h2 % 2}", r=["ZS"], w=[f"gch{h2 % 2}"])
            S.tt("dve", ybf[0:64], osq[0:64], sgl[0:64], ALU.mult, r=["osq", "sgl"], w=["ybf"])
            for half in range(2):
                bank = 6 + half
                pv = ps16(bank)
                for n8 in range(8):
                    n = half * 8 + n8
                    S.tr(pv[:, n8 * 64:(n8 + 1) * 64], ybf[0:64, n, :], ident[0:64, 0:64],
                         r=["ybf", "ident"], w=[f"ps{bank}"] if n8 in (0, 7) else [])
                S.copy(evac_eng(), yT[:, 8 + h, half * 512:(half + 1) * 512], pv[:, 0:512], r=[f"ps{bank}"], w=["yT"])

        p2_stage_a(0)
        for h in range(8):
            if h + 1 < 8:
                p2_stage_a(h + 1)
            p2_stage_b(h)
        S.barrier(skip=("cc_",))
        if upto < 2:
            break
        A.reset(wo_mark)
        ystage = A.alloc([32, 256], BF16)
        QK = A.alloc([16, 1024], BF16)
        VA = A.alloc([64, 128], BF16)
        V1 = A.alloc([32, 129], BF16)
        fl2 = A.alloc([32, 2], F32)
        bsel = A.alloc([2], F32)
        nbs = A.alloc([2], F32)
        pos = A.alloc([32], F32)
        e1 = A.alloc([32], F32)
        csT = A.alloc([128], F32)
        Cs = A.alloc([32], F32)
        offs = A.alloc([32], F32)
        bIJ = A.alloc([32, 32], F32)
        PT = [A.alloc([128], BF16) for _ in range(6)]
        rec = [A.alloc([1], F32) for _ in range(2)]
        gz_x = GZ[l].ap().rearrange("(rp k x p) t -> p rp k x t", rp=4, k=5, x=16, p=128)
        gz_tok = GZ[l].ap().rearrange("(rp k r2) (y c) -> rp k (r2 y) c", rp=4, k=5, c=128)
        gz_g128 = gz_tok.rearrange("rp k (g p) c -> p rp k g c", p=128)
        gz_fa = GZ[l].ap().bitcast(F32).rearrange("(rp k x) c -> rp k (x c)", rp=4, k=5)
        S.dma("sp", QK, lambda rv: gz_x[:, bass.ds(rv[0], 1), 0, :, :].rearrange("p a x t -> p (a x) t"),
              "QK", r=["GZk0"], w=["QK"])
        S.dma("sp", VA, lambda rv: gz_g128[:, bass.ds(rv[0], 1), 1, 0:64, :].rearrange("p a g c -> p (a g) c"),
              "VA", r=["GZk1"], w=["VA"])
        S.dma("sp", fl2, lambda rv: gz_fa[bass.ds(rv[0], 1), 4, 0:8192].rearrange("a (g p j) -> p (a g) j", p=128, j=2),
              "fl2", r=["GZk4"], w=["fl2"])
        S.dma("sp", bsel, (lambda l_: (lambda rv: b_fox_f[l_, :].partition_broadcast(128)[:, bass.ds(2 * rv[0], 2)]))(l),
              "bsel", w=["bsel"])
        wo_v = w_out[l].rearrange("(kc p) n -> p kc n", p=128)
        for nb in range(4):
            S.dma("pool", wo[:, :, nb * 512:(nb + 1) * 512], wo_v[:, :, nb * 512:(nb + 1) * 512], "wo",
                  r=["QK", "VA", "fl2"], w=["wo"])
        S.memset("dve", V1[:, :, 128:129], 1.0, w=["V1"])
        VA5 = VA.rearrange("p (rho s i) c -> p rho s i c", rho=4, s=2)
        S.ts("dve", nbs, bsel, -1.0, None, ALU.mult, r=["bsel"], w=["nbs"])
        for j in range(2):
            S.copy("pool", V1[:, :, 0:128].rearrange("p (rho i) c -> p rho i c", rho=4), VA5[:, :, j, :, :],
                   r=["VA"], w=["V1"])
            S.act(e1, fl2[:, :, j], AF.Exp, bias=nbs[:, j:j + 1], scale=-1.0, r=["fl2", "nbs"], w=["e1"])
            S.act(pos, e1, AF.Ln, bias=onec, scale=1.0, r=["e1", "onec"], w=["pos"])
            S.mm(ps32(4)[0:32, 0:128], pos, ones, r=["pos", "ones"], w=["ps4"])
            S.copy("dve", csT[0:32, :], ps32(4)[0:32, 0:128], r=["ps4"], w=["csT"])
            S.mm(ps32(5)[:, 0:32], tri, pos, start=True, stop=False, r=["tri", "pos"], w=["ps5"])
            S.mm(ps32(5)[:, 0:32], csT[0:32, :], striu[0:32, :], start=False, stop=True, r=["csT", "striu"], w=["ps5"])
            S.mm(ps32(5)[:, 64:96], csT[0:32, :], striu[0:32, :], start=True, stop=True, r=["csT", "striu"], w=["ps5"])
            S.copy("dve", Cs, ps32(5)[:, 0:32], r=["ps5"], w=["Cs"])
            S.copy("dve", offs, ps32(5)[:, 64:96], r=["ps5"], w=["offs"])
            for I in range(32):
                S.ts("dve", bIJ[:, I, :], Cs, offs[:, I:I + 1], None, ALU.subtract, r=["Cs", "offs"], w=["bIJ"])
            pairs = [(I, J) for I in range(32) for J in range(I + 1)]
            LA = 3
            STB = [0, 1, 4, 5]
            NPT = len(PT)
            for n in range(len(pairs) + LA):
                if n < len(pairs):
                    I, J = pairs[n]
                    sbk = STB[n % 4]
                    pt = n % NPT
                    qsl = QK[:, (I // 8) * 4 + j, (I % 8) * 128:(I % 8 + 1) * 128]
                    ksl = QK[:, (J // 8) * 4 + 2 + j, (J % 8) * 128:(J % 8 + 1) * 128]
                    S.mm(ps32(sbk)[:, 0:128], ksl, qsl, r=["QK"], w=[f"ps{sbk}"])
                    S.act(PT[pt], ps32(sbk)[:, 0:128], AF.Exp, bias=bIJ[:, I, J:J + 1], scale=SCALE,
                          r=[f"ps{sbk}", "bIJ"], w=[f"PT{pt}"])
                    if J == I:
                        S.tt("dve", PT[pt], PT[pt], maskT, ALU.mult, r=[f"PT{pt}", "maskT"], w=[f"PT{pt}"])
                m_ = n - LA
                if m_ >= 0:
                    I, J = pairs[m_]
                    pt = m_ % NPT
                    ob = 2 + I % 2
                    S.mm(ps32(ob)[:, 0:129], PT[pt], V1[:, J, :], start=(J == 0), stop=(J == I),
                         r=[f"PT{pt}", "V1"], w=[f"ps{ob}"] if J in (0, I) else [])
                    if J == I:
                        rc = rec[I % 2]
                        S.op("dve", (lambda rc_, ob_: (lambda e: e.reciprocal(rc_, ps32(ob_)[:, 128:129])))(rc, ob),
                             [f"ps{ob}"], [f"rec{I % 2}"])
                        hf = I // 8
                        S.ts("dve", ystage[:, I, j * 128:(j + 1) * 128], ps32(ob)[:, 0:128], rc, None, ALU.mult,
                             r=[f"ps{ob}", f"rec{I % 2}"], w=[f"ystage{j}{hf}"])
                        if I % 8 == 7:
                            S.dma("sp", ysA[j].ap().rearrange("(J p) c -> p J c", p=128)[:, hf * 8:(hf + 1) * 8, :],
                                  ystage[:, hf * 8:(hf + 1) * 8, j * 128:(j + 1) * 128], f"ystage{j}{hf}",
                                  r=[f"ystage{j}{hf}"], w=[f"ysA{j}{hf}"])
                            allgather(ysA[j][hf * 1024:(hf + 1) * 1024, :], GYA[j][hf * 81920:hf * 81920 + 4096, :],
                                      f"ysA{j}{hf}", f"GYA{j}", "cc_ya")
        S.barrier(skip=("cc_",))
        if upto < 3:
            break
        A.reset(base_mark)
        yT = A.alloc([KC, NTOK], BF16)
        wo = A.alloc([KC, D], BF16)
        YTA = A.alloc([32, 256], BF16)
        mixsb = [A.alloc([D], F32) for _ in range(2)]
        hts = [A.alloc([D], F32) for _ in range(2)]
        junk = A.alloc([D], BF16)
        st_ss = [A.alloc([1], F32) for _ in range(6)]
        for jj_ in range(2):
            gya_v = GYA[jj_].ap().rearrange("(rp g p) c -> p rp g c", rp=4, p=128)
            S.dma("sp", YTA[:, :, jj_ * 128:(jj_ + 1) * 128],
                  (lambda V_: (lambda rv: V_[:, bass.ds(rv[0], 1), 0:32, :].rearrange("p a g c -> p (a g) c")))(gya_v),
                  "YTA", r=[f"GYA{jj_}"], w=["YTA"])
        for i in range(NT):
            pv = ps16(6 + i % 2)
            bank = 6 + i % 2
            for k8 in range(8):
                m, jj = k8 // 2, k8 % 2
                S.tr(pv[:, k8 * 128:(k8 + 1) * 128], YTA[:, m * 8 + i, jj * 128:(jj + 1) * 128], ident,
                     r=["YTA", "ident"], w=[f"ps{bank}"] if k8 in (0, 7) else [])
            S.copy(evac_eng(), yT[:, 0:8, i * 128:(i + 1) * 128], pv.rearrange("p (a b) -> p a b", b=128),
                   r=[f"ps{bank}"], w=["yT"])
        gain, gkey = load_gain(g_mix_post, l)
        gain2, gkey2 = load_gain(g_ffn_pre, l)
        hnq = [A.alloc([D], BF16) for _ in range(2)]
        st_q = [A.alloc([1], F32) for _ in range(6)]
        junkq = A.alloc([D], BF16)
        for i in range(NT):
            b = i % 2
            for nb in range(4):
                bank = nb + 4 * (i % 2)
                for kc in range(KC):
                    S.mm(ps32(bank), yT[:, kc, i * 128:(i + 1) * 128], wo[:, kc, nb * 512:(nb + 1) * 512],
                         start=(kc == 0), stop=(kc == KC - 1), r=["yT", "wo"], w=[f"ps{bank}"] if kc in (0, KC - 1) else [])
                S.copy(evac_eng(), mixsb[b][:, nb * 512:(nb + 1) * 512], ps32(bank), r=[f"ps{bank}"], w=[f"mixsb{b}"])
            dsts = [hbuf[i * 128:(i + 1) * 128, :]]
            resid_update(mixsb[b], [f"mixsb{b}"], gain, gkey, hsrc[i * 128:(i + 1) * 128, :], dsts,
                         (junk, st_ss[0 + b], st_ss[2 + b], st_ss[4 + b], hts[b], f"p3a{b}"), f"hbuf{i}")
            if i == NT - 1:
                S.dma("sp", hh[l][:, :], hts[b][126:128, :], f"p3a{b}ht", r=[f"p3a{b}ht"], w=["hh"])
            _norm_stats(hts[b], [f"p3a{b}ht"], 128, gain2, gkey2,
                        (junkq, st_q[0 + b], st_q[2 + b], st_q[4 + b], hnq[b], f"p3aq{b}"))
            S.dma("sp", hn2buf[i * 128:(i + 1) * 128, :], hnq[b], f"p3aq{b}hnb", r=[f"p3aq{b}hnb"], w=[f"hn2buf{i}"])
        allgather(hh[l].ap(), ghh[l].ap(), "hh", "ghh", "cc_h")
        if debug and l == 0 and upto == 3:
            dbg_out("hbuf_a", hbuf.ap())
        S.barrier()
        if upto < 4:
            break
        A.reset(base_mark)
        hn2T = A.alloc([KC, NTOK + 2], BF16)
        a_seg = A.alloc([NFB, 512], BF16)
        cw = cw_all[l]
        cb = cb_all[l]
        uhalo = A.alloc([88, 2], F32)
        junk = A.alloc([D], BF16)
        st_ss = [A.alloc([1], F32) for _ in range(6)]
        hn2t2 = A.alloc([2, D], BF16)
        hn2t = [hn2t2[:, 0, :], hn2t2[:, 1, :]]
        fm = A.mark()
        gain, gkey = load_gain(g_ffn_pre, l)
        for i in range(NT):
            b = i % 2
            S.dma("sp", hn2t[b], hn2buf[i * 128:(i + 1) * 128, :], f"p3bn{b}hnb", r=[f"hn2buf{i}"], w=[f"p3bn{b}hnb"])
            _norm_tr(128, hn2T, "hn2T", 2 + i * 128, (junk, None, None, None, hn2t[b], f"p3bn{b}"))
        a_flat = a_seg.rearrange("p a b -> p (a b)")
        hl_f = a_flat[:, 0:4096].bitcast(F32)
        hl_b = a_flat[:, 4096:6144]
        hl_j = a_flat[:, 6144:8192]
        S.dma("sp", hl_f[0:2, :], (lambda l_: (lambda rv: ghh[l_][bass.ds(((rv[0] + 3) % 4) * 2, 2), :]))(l), "p3bhalo", r=["ghh"], w=["a_seg"])
        S.ts("dve", hl_f[0:2, :], hl_f[0:2, :], hmask[0:2, :], None, ALU.mult, r=["a_seg", "hmask"], w=["a_seg"])
        S.act(hl_j[0:2, :], hl_f[0:2, :], AF.Square, accum=st_ss[0][0:2, :], r=["a_seg"], w=["a_seg", "p3bh_ss"])
        S.act(st_ss[2][0:2, :], st_ss[0][0:2, :], AF.Sqrt, bias=epsc[0:2, :], scale=1.0 / D, r=["p3bh_ss", "epsc"], w=["p3bh_sd"])
        S.op("dve", (lambda a_, b_: (lambda e: e.reciprocal(a_, b_)))(st_ss[4][0:2, :], st_ss[2][0:2, :]),
             ["p3bh_sd"], ["p3bh_rs"])
        S.stt("dve", hl_b[0:2, :], hl_f[0:2, :], st_ss[4][0:2, :], gain[0:2, :], ALU.mult, ALU.mult,
              r=["a_seg", "p3bh_rs", gkey], w=["a_seg"])
        for half in range(2):
            bank = 6 + half
            pv = ps16(bank)
            for k8 in range(8):
                kc = half * 8 + k8
                S.tr(pv[:, k8 * 128:k8 * 128 + 2], hl_b[0:2, kc * 128:(kc + 1) * 128], ident[0:2, 0:2],
                     r=["a_seg", "ident"], w=[f"ps{bank}"] if k8 in (0, 7) else [])
            S.copy(evac_eng(), hn2T[:, half * 8:(half + 1) * 8, 0:2], pv.rearrange("p (a b) -> p a b", b=128)[:, :, 0:2],
                   r=[f"ps{bank}"], w=["hn2T"])
        gainp, gkeyp = load_gain(g_ffn_post, l)
        wu_v = w_up[l].rearrange("(kc p) n -> p kc n", p=128)
        wd_v = w_down[l].rearrange("(fc p) n -> p fc n", p=128)
        for sgm in range(2):
            A.reset(fm)
            wus = [A.alloc([KC, 1024], BF16) for _ in range(2)]
            ubuf = [A.alloc([514], F32) for _ in range(4)]
            cbuf = [A.alloc([512], F32) for _ in range(4)]
            pbk = 0
            for gq in range(11):
                sl = gq % 2
                if not (sgm == 1 and gq == 0):
                    S.dma("pool", wus[sl][:, :, 0:512], wu_v[:, :, gq * 512:(gq + 1) * 512], f"wus{sl}", w=[f"wus{sl}"])
                    S.dma("pool", wus[sl][:, :, 512:1024], wu_v[:, :, D_FF + gq * 512:D_FF + (gq + 1) * 512], f"wus{sl}", w=[f"wus{sl}"])
                for jj in range(4):
                    fbg = gq * 4 + jj
                    cbs = []
                    for which in range(2):
                        fb88 = fbg + which * NFB
                        wc = which * 512 + jj * 128
                        ub = ubuf[(2 * fbg + which) % 4]
                        cbf = cbuf[(2 * fbg + which) % 4]
                        uk = f"ubuf{(2 * fbg + which) % 4}"
                        ck = f"cbuf{(2 * fbg + which) % 4}"
                        bank = pbk % 4
                        pbk += 1
                        if sgm == 0:
                            for kc in range(KC):
                                S.mm(ps32(4 + bank % 2)[:, 0:2], wus[sl][:, kc, wc:wc + 128], hn2T[:, kc, 0:2],
                                     start=(kc == 0), stop=(kc == KC - 1), r=[f"wus{sl}", "hn2T"],
                                     w=[f"ps{4 + bank % 2}"] if kc in (0, KC - 1) else [])
                            S.copy("act", ub[:, 0:2], ps32(4 + bank % 2)[:, 0:2], r=[f"ps{4 + bank % 2}"], w=[uk])
                        else:
                            S.copy("dve", ub[:, 0:2], uhalo[:, fb88, :], r=["uhalo"], w=[uk])
                        for kc in range(KC):
                            S.mm(ps32(bank), wus[sl][:, kc, wc:wc + 128], hn2T[:, kc, 2 + sgm * 512:2 + (sgm + 1) * 512],
                                 start=(kc == 0), stop=(kc == KC - 1), r=[f"wus{sl}", "hn2T"],
                                 w=[f"ps{bank}"] if kc in (0, KC - 1) else [])
                        S.copy("act", ub[:, 2:514], ps32(bank), r=[f"ps{bank}"], w=[uk])
                        S.act(cbf, ps32(bank), AF.Identity, bias=cb[:, fb88:fb88 + 1], scale=cw[:, 2, fb88:fb88 + 1],
                              r=[f"ps{bank}", "cw", "cb"], w=[ck])
                        S.stt("dve", cbf, ub[:, 1:513], cw[:, 1, fb88:fb88 + 1], cbf, ALU.mult, ALU.add, r=[uk, ck, "cw"], w=[ck])
                        S.stt("dve", cbf, ub[:, 0:512], cw[:, 0, fb88:fb88 + 1], cbf, ALU.mult, ALU.add, r=[uk, ck, "cw"], w=[ck])
                        if sgm == 0:
                            S.copy("dve", uhalo[:, fb88, :], ub[:, 512:514], r=[uk], w=["uhalo"])
                        cbs.append((cbf, ck))
                    (cg, cgk), (cu, cuk) = cbs
                    S.act(cg, cg, AF.Silu, r=[cgk], w=[cgk])
                    S.tt("dve", a_seg[:, fbg, :], cg, cu, ALU.mult, r=[cgk, cuk], w=["a_seg"])
            S.barrier()
            A.reset(fm)
            wds = [A.alloc([22, 512], BF16) for _ in range(2)]
            ffsb = A.alloc([4, D], F32)
            hts = [hn2t2.rearrange("p a b -> p (a b)").bitcast(F32), A.alloc([D], F32)]
            for nb in range(4):
                for half in range(2):
                    S.dma("pool", wds[half], wd_v[:, half * 22:(half + 1) * 22, nb * 512:(nb + 1) * 512], f"wds{half}", w=[f"wds{half}"])
                    for t4 in range(4):
                        for fcl in range(22):
                            fc = half * 22 + fcl
                            S.mm(ps32(t4), a_seg[:, fc, t4 * 128:(t4 + 1) * 128], wds[half][:, fcl, :],
                                 start=(fc == 0), stop=(fc == NFB - 1), r=["a_seg", f"wds{half}"],
                                 w=[f"ps{t4}"] if fc in (0, NFB - 1) else [])
                for t4 in range(4):
                    S.copy(evac_eng(), ffsb[:, t4, nb * 512:(nb + 1) * 512], ps32(t4), r=[f"ps{t4}"], w=[f"ffsb{t4}"])
            if sgm == 0:
                S.dma("pool", wus[0][:, :, 0:512], wu_v[:, :, 0:512], "wus0", w=["wus0", "wds0", "wds1"])
                S.dma("pool", wus[0][:, :, 512:1024], wu_v[:, :, D_FF:D_FF + 512], "wus0", w=["wus0", "wds0", "wds1"])
            for t4 in range(4):
                i = sgm * 4 + t4
                b = t4 % 2
                resid_update(ffsb[:, t4, :], [f"ffsb{t4}"], gainp, gkeyp, hbuf[i * 128:(i + 1) * 128, :],
                             [hbuf[i * 128:(i + 1) * 128, :]],
                             (junk, st_ss[0 + b], st_ss[2 + b], st_ss[4 + b], hts[b], f"p3d{b}"), f"hbuf{i}",
                             alias=["p3bn0hnb", "p3bn1hnb"] if b == 0 else [])
            S.barrier()
        if debug and l == 0 and upto == 4:
            dbg_out("hbuf_b", hbuf.ap())
        if upto < 5:
            break
        A.reset(base_mark)
        hn3T = A.alloc([KC, NTOK], BF16)
        pT = A.alloc([2, NTOK], BF16)
        wg = A.alloc([KC, D], BF16)
        wp = A.alloc([2, D], BF16)
        junk = A.alloc([D], BF16)
        junk2 = A.alloc([D], BF16)
        st_ss = [A.alloc([1], F32) for _ in range(6)]
        st_s2 = [A.alloc([1], F32) for _ in range(6)]
        ht = [A.alloc([D], F32) for _ in range(2)]
        hnb = [A.alloc([D], BF16) for _ in range(2)]
        ptk = [A.alloc([256], F32) for _ in range(2)]
        pbf = [A.alloc([256], BF16) for _ in range(2)]
        sgs = [A.alloc([512], F32) for _ in range(2)]
        egs = [A.alloc([D], F32) for _ in range(2)]
        wg_v = w_ple_gate[l].rearrange("(kc p) n -> p kc n", p=128)
        wp_v = w_ple_proj[l].rearrange("(kc p) n -> p kc n", p=128)
        for nb in range(4):
            S.dma("pool", wg[:, :, nb * 512:(nb + 1) * 512], wg_v[:, :, nb * 512:(nb + 1) * 512], "wg", w=["wg"])
        S.dma("pool", wp, wp_v, "wp", w=["wp"])
        gain, gkey = load_gain(g_ple_in, l)
        gainp, gkeyp = load_gain(g_ple_post, l)
        def ple_front(i):
            b = i % 2
            S.dma("sp", ht[b], hbuf[i * 128:(i + 1) * 128, :], f"p3cht{b}", r=[f"hbuf{i}"], w=[f"p3cht{b}"])
            S.dma("sp", ptk[b], p_in[l, i * 128:(i + 1) * 128, :], f"ptk{b}", w=[f"ptk{b}"])
            norm_transpose(ht[b], [f"p3cht{b}"], 128, gain, gkey, hn3T, "hn3T", i * 128,
                           (junk, st_ss[0 + b], st_ss[2 + b], st_ss[4 + b], hnb[b], f"p3cn{b}"), part=1)
            S.copy("dve", pbf[b], ptk[b], r=[f"ptk{b}"], w=[f"pbf{b}"])

        def ple_mid(i):
            b = i % 2
            norm_transpose(ht[b], [f"p3cht{b}"], 128, gain, gkey, hn3T, "hn3T", i * 128,
                           (junk, st_ss[0 + b], st_ss[2 + b], st_ss[4 + b], hnb[b], f"p3cn{b}"), part=2)
            pv = ps16(6)
            for k2 in range(2):
                S.tr(pv[:, k2 * 128:(k2 + 1) * 128], pbf[b][:, k2 * 128:(k2 + 1) * 128], ident,
                     r=[f"pbf{b}", "ident"], w=["ps6"])
            S.copy("act", pT[:, :, i * 128:(i + 1) * 128], pv[:, 0:256].rearrange("p (a b) -> p a b", b=128),
                   r=["ps6"], w=["pT"])

        ple_front(0)
        ple_mid(0)
        for i in range(NT):
            b = i % 2
            if i + 1 < NT:
                ple_front(i + 1)
            for nb in range(4):
                gbk = nb
                ebk = 4 + nb % 2
                for kc in range(KC):
                    S.mm(ps32(gbk), hn3T[:, kc, i * 128:(i + 1) * 128], wg[:, kc, nb * 512:(nb + 1) * 512],
                         start=(kc == 0), stop=(kc == KC - 1), r=["hn3T", "wg"], w=[f"ps{gbk}"] if kc in (0, KC - 1) else [])
                for k2 in range(2):
                    S.mm(ps32(ebk), pT[:, k2, i * 128:(i + 1) * 128], wp[:, k2, nb * 512:(nb + 1) * 512],
                         start=(k2 == 0), stop=(k2 == 1), r=["pT", "wp"], w=[f"ps{ebk}"])
                sg = sgs[nb % 2]
                S.act(sg, ps32(gbk), AF.Sigmoid, r=[f"ps{gbk}"], w=[f"sgs{nb % 2}"])
                S.tt("dve", egs[b][:, nb * 512:(nb + 1) * 512], ps32(ebk), sg, ALU.mult,
                     r=[f"ps{ebk}", f"sgs{nb % 2}"], w=[f"egs{b}"])
            if i + 1 < NT:
                ple_mid(i + 1)
            tag = f"p3ce{b}"
            ss, sd, rstd = st_s2[0 + b], st_s2[2 + b], st_s2[4 + b]
            S.act(junk2, egs[b], AF.Square, accum=ss, r=[f"egs{b}"], w=[tag + "junk", tag + "ss"])
            S.act(sd, ss, AF.Sqrt, bias=epsc, scale=1.0 / D, r=[tag + "ss", "epsc"], w=[tag + "sd"])
            S.op("dve", (lambda a_, b_: (lambda e: e.reciprocal(a_, b_)))(rstd, sd), [tag + "sd"], [tag + "rstd"])
            S.stt("dve", egs[b], egs[b], rstd, gainp, ALU.mult, ALU.mult, r=[f"egs{b}", tag + "rstd", gkeyp], w=[f"egs{b}"])
            S.tt("dve", egs[b], egs[b], ht[b], ALU.add, r=[f"egs{b}", f"p3cht{b}"], w=[f"egs{b}"])
            dst = out if l == DEPTH - 1 else hbuf
            S.dma("sp", dst[i * 128:(i + 1) * 128, :], egs[b], f"egs{b}", r=[f"egs{b}"], w=[f"hbuf{i}"])
        S.barrier()
        if upto < 6:
            break

    for name, (t, src) in dbg.items():
        S.dma("sp", t.ap(), src, "dbg_" + name, r=[], w=[])
    S.barrier()
    with ExitStack() as stack:
        S.replay(stack)
    return nc, list(dbg.keys())


def make_consts():
    idx = np.arange(128)
    c = {}
    c["c_ident"] = np.eye(128, dtype=np.float32)
    c["c_maskT"] = (idx[None, :] >= idx[:, None]).astype(np.float32)
    c["c_ones"] = np.ones((128, 128), np.float32)
    j = np.arange(32)
    c["c_striu"] = (j[:, None] < j[None, :]).astype(np.float32)
    seg = np.ones((128, 1024), np.float32)
    seg[:, ::64] = 0.0
    c["c_seg"] = seg
    return c


_CACHE = {}


def run(inputs, debug=False, upto=99, trace=False):
    key = (debug, upto)
    if key not in _CACHE:
        _CACHE[key] = build_program(debug=debug, upto=upto)
    nc, dbgnames = _CACHE[key]
    consts = make_consts()
    x = np.asarray(inputs["x"], np.float32)
    p = np.asarray(inputs["p"], np.float32)
    in_maps = []
    wnames = ["g_mix_pre", "w_in", "b_fox_f", "w_hgrn_lb", "g_hgrn_out", "w_out", "g_mix_post", "g_ffn_pre",
              "w_up", "conv_w", "conv_b", "w_down", "g_ffn_post", "g_ple_in", "w_ple_gate", "w_ple_proj", "g_ple_post"]
    shared = {n: np.ascontiguousarray(np.asarray(inputs[n], np.float32)) for n in wnames}
    for c in range(8):
        b, r = c // 4, c % 4
        m = dict(shared)
        m.update(consts)
        m["x"] = np.ascontiguousarray(x[b, r * NTOK:(r + 1) * NTOK, :])
        m["p"] = np.ascontiguousarray(p[:, b, r * NTOK:(r + 1) * NTOK, :])
        m["c_hmask"] = np.full((128, 1), 1.0 if r > 0 else 0.0, np.float32)
        sel = np.zeros((128, 4), np.float32)
        sel[:, r] = 1.0
        m["c_sel"] = sel
        in_maps.append(m)
    res = run_bass_kernel_spmd(nc, in_maps, core_ids=list(range(8)), trace=trace)
    return res, dbgnames


def kernel(**inputs):
    res, _ = run(inputs)
    outp = np.zeros((2, SEQ, D), np.float32)
    for c in range(8):
        b, r = c // 4, c % 4
        outp[b, r * NTOK:(r + 1) * NTOK, :] = res.results[c]["out"]
    return outp
```

```python
import numpy as np
import concourse.bass as bass
import concourse.mybir as mybir
from concourse.bass_utils import run_bass_kernel_spmd
from contextlib import ExitStack

F32 = mybir.dt.float32
BF16 = mybir.dt.bfloat16
U8 = mybir.dt.uint8
AF = mybir.ActivationFunctionType
ALU = mybir.AluOpType
AX = mybir.AxisListType

DEPTH = 2
D = 2048
NTOK = 1024
NT = 8
KC = 16
SEQ = 4096
N_IN = 7176
D_FF = 5632
NFB = 44
EPS = 1e-6
SCALE = 128 ** -0.5
ENGS = ["pe", "act", "dve", "pool", "sp"]


class Sched:
    def __init__(self, nc):
        self.nc = nc
        self.ops = {e: [] for e in ENGS}
        self.tick = {e: 0 for e in ENGS}
        self.seen = {e: {} for e in ENGS}
        self.lastw = {}
        self.readers = {}
        self.cnt = {}
        self.semnames = set("c_" + e for e in ENGS)
        self.rv = {}

    def op(self, eng, fn, r=(), w=(), dma=None, inc=None):
        own = "c_" + eng
        waits = {}

        def need(sem, val, war):
            if sem == own and dma is None:
                if eng == "pe" or war:
                    return
            if self.seen[eng].get(sem, 0) >= val:
                return
            if waits.get(sem, 0) < val:
                waits[sem] = val

        for k in r:
            for s, v in self.lastw.get(k, {}).items():
                need(s, v, False)
        for k in w:
            for s, v in self.lastw.get(k, {}).items():
                need(s, v, False)
            for s, v in self.readers.get(k, {}).items():
                need(s, v, True)
        for s, v in waits.items():
            self.seen[eng][s] = v
        if dma is not None:
            sem = dma
            step = 16 if inc is None else inc
            self.cnt[sem] = self.cnt.get(sem, 0) + step
            val = self.cnt[sem]
            self.semnames.add(sem)
        else:
            sem = own
            step = 1
            self.tick[eng] += 1
            val = self.tick[eng]
        for k in w:
            self.lastw.setdefault(k, {})[sem] = val
            self.readers[k] = {}
        for k in r:
            self.readers.setdefault(k, {})[sem] = val
        self.ops[eng].append((sorted(waits.items()), fn, sem, step))

    def barrier(self, skip=()):
        allv = {("c_" + e): self.tick[e] for e in ENGS if self.tick[e] > 0}
        allv.update({k_: v_ for k_, v_ in self.cnt.items() if not (skip and k_.startswith(tuple(skip)))})
        for e in ENGS:
            waits = []
            for s, v in allv.items():
                if s == "c_" + e:
                    continue
                if self.seen[e].get(s, 0) < v:
                    waits.append((s, v))
                    self.seen[e][s] = v
            if waits:
                self.ops[e].append((sorted(waits), None, None, 0))

    def mm(self, out, lhsT, rhs, start=True, stop=True, r=(), w=()):
        self.op("pe", lambda e: e.matmul(out, lhsT, rhs, start=start, stop=stop), r, w)

    def tr(self, out, in_, ident, r=(), w=()):
        self.op("pe", lambda e: e.transpose(out, in_, ident), r, w)

    def act(self, out, in_, func, bias=None, scale=None, accum=None, r=(), w=()):
        kw = {}
        if bias is not None:
            kw["bias"] = bias
        if scale is not None:
            kw["scale"] = scale
        if accum is not None:
            kw["accum_out"] = accum
        self.op("act", lambda e: e.activation(out, in_, func, **kw), r, w)

    def ts(self, eng, out, in0, s1, s2, op0, op1=None, r=(), w=()):
        if op1 is None:
            self.op(eng, lambda e: e.tensor_scalar(out, in0, s1, None, op0), r, w)
        else:
            self.op(eng, lambda e: e.tensor_scalar(out, in0, s1, s2, op0, op1), r, w)

    def tt(self, eng, out, in0, in1, op, r=(), w=()):
        self.op(eng, lambda e: e.tensor_tensor(out, in0, in1, op), r, w)

    def stt(self, eng, out, in0, scalar, in1, op0, op1, r=(), w=()):
        self.op(eng, lambda e: e.scalar_tensor_tensor(out, in0, scalar, in1, op0, op1), r, w)

    def copy(self, eng, out, in_, r=(), w=()):
        if eng == "act":
            self.op(eng, lambda e: e.activation(out, in_, AF.Identity), r, w)
        else:
            self.op(eng, lambda e: e.tensor_copy(out, in_), r, w)

    def memset(self, eng, ap, val, w=()):
        self.op(eng, lambda e: e.memset(ap, val), (), w)

    def dma(self, eng, out, in_, sem, r=(), w=(), slow=False):
        def fn(e):
            rvx = RV(self.rv[eng], e) if (callable(out) or callable(in_)) else None
            o = out(rvx) if callable(out) else out
            i = in_(rvx) if callable(in_) else in_
            try:
                if slow:
                    return e.dma_start(out=o, in_=i, allow_slow_non_contiguous=True)
                return e.dma_start(out=o, in_=i)
            except Exception:
                print("DMA FAIL", sem, o.shape, o.ap, i.shape, i.ap, flush=True)
                raise
        self.op(eng, fn, r, w, dma="d_" + sem)

    def replay(self, stack):
        nc = self.nc
        sems = {}
        for n in sorted(self.semnames):
            sems[n] = stack.enter_context(nc.semaphore(n))
        block = stack.enter_context(nc.Block())

        def mk(en):
            def body(e):
                if en in ("sp", "pool"):
                    r_ = e.snap(e.partition_id() % 4)
                    self.rv[en] = (r_, e.snap(r_ // 2), e.snap(r_ % 2))
                for waits, fn, sem, step in self.ops[en]:
                    for s, v in waits:
                        e.wait_ge(sems[s], v)
                    if fn is not None:
                        fn(e).then_inc(sems[sem], step)
            return body

        block.tensor(mk("pe"))
        block.scalar(mk("act"))
        block.vector(mk("dve"))
        block.gpsimd(mk("pool"))
        block.sync(mk("sp"))


class RV:
    def __init__(self, vals, e):
        self.vals = vals
        self.e = e

    def __getitem__(self, i):
        return self.vals[i]

    def sn(self, expr):
        return self.e.snap(expr, donate=True)


class Arena:
    def __init__(self, nc, nbytes):
        self.t = nc.alloc_sbuf_tensor("arena", [128, nbytes], U8)
        self.nbytes = nbytes
        self.off = 0

    def mark(self):
        return self.off

    def reset(self, m):
        self.off = m

    def alloc(self, shape, dtype, npart=128):
        esz = 4 if dtype == F32 else 2
        n = 1
        for s in shape:
            n *= s
        nb = (n * esz + 31) // 32 * 32
        assert self.off + nb <= self.nbytes, (self.off, nb, self.nbytes)
        ap = self.t[0:npart, self.off:self.off + nb // 1]
        ap = self.t[0:npart, self.off:self.off + n * esz].bitcast(dtype)
        self.off += nb
        if len(shape) == 2:
            ap = ap.rearrange("p (a b) -> p a b", b=shape[1])
        elif len(shape) == 3:
            ap = ap.rearrange("p (a b c) -> p a b c", b=shape[1], c=shape[2])
        return ap


def build_program(debug=False, upto=99):
    nc = bass.Bass("TRN2", target_bir_lowering=False)
    S = Sched(nc)
    dt_in = {}

    def inp(name, shape):
        dt_in[name] = nc.dram_tensor(name, shape, F32, kind="ExternalInput")
        return dt_in[name]

    x = inp("x", [NTOK, D])
    p_in = inp("p", [DEPTH, NTOK, 256])
    g_mix_pre = inp("g_mix_pre", [DEPTH, D])
    w_in = inp("w_in", [DEPTH, D, N_IN])
    b_fox_f = inp("b_fox_f", [DEPTH, 8])
    w_hgrn_lb = inp("w_hgrn_lb", [DEPTH, 1024])
    g_hgrn_out = inp("g_hgrn_out", [DEPTH, 128])
    w_out = inp("w_out", [DEPTH, D, D])
    g_mix_post = inp("g_mix_post", [DEPTH, D])
    g_ffn_pre = inp("g_ffn_pre", [DEPTH, D])
    w_up = inp("w_up", [DEPTH, D, 2 * D_FF])
    conv_w = inp("conv_w", [DEPTH, 3, 2 * D_FF])
    conv_b = inp("conv_b", [DEPTH, 2 * D_FF])
    w_down = inp("w_down", [DEPTH, D_FF, D])
    g_ffn_post = inp("g_ffn_post", [DEPTH, D])
    g_ple_in = inp("g_ple_in", [DEPTH, D])
    w_ple_gate = inp("w_ple_gate", [DEPTH, D, D])
    w_ple_proj = inp("w_ple_proj", [DEPTH, 256, D])
    g_ple_post = inp("g_ple_post", [DEPTH, D])
    c_ident = inp("c_ident", [128, 128])
    c_maskT = inp("c_maskT", [128, 128])
    c_ones = inp("c_ones", [128, 128])
    c_striu = inp("c_striu", [32, 32])
    c_seg = inp("c_seg", [128, 1024])
    c_hmask = inp("c_hmask", [128, 1])
    c_sel = inp("c_sel", [128, 4])
    out = nc.dram_tensor("out", [NTOK, D], F32, kind="ExternalOutput")

    hbuf = nc.dram_tensor("hbuf", [NTOK, D], F32)
    hn2buf = nc.dram_tensor("hn2buf", [NTOK, D], BF16)
    ZS = [nc.dram_tensor("ZS", [20 * 512, 1024], BF16)] * DEPTH
    GZ = [nc.dram_tensor("GZ", [20 * 2048, 1024], BF16)] * DEPTH
    ysA = [nc.dram_tensor(f"ysA{j}", [SEQ, 128], BF16) for j in range(2)]
    OH = nc.dram_tensor("OH", [8 * 1024, 128], F32)
    ST = nc.dram_tensor("ST", [1032, 128], F32)
    GST = nc.dram_tensor("GST", [4 * 1032, 128], F32)
    GYA = [nc.dram_tensor(f"GYA{j}", [4 * 81920, 128], BF16) for j in range(2)]
    hh = [nc.dram_tensor("hh", [2, D], F32)] * DEPTH
    ghh = [nc.dram_tensor("ghh", [8, D], F32)] * DEPTH
    GROUPS = [[0, 1, 2, 3], [4, 5, 6, 7]]
    import os
    NOAG = bool(os.environ.get("K_NOAG"))
    agn = [0]

    def allgather(sa, da, rkey, wkey, semname="cc_z"):
        if NOAG:
            return
        agn[0] += 1
        S.op("pool", lambda e: e.collective_compute("AllGather", ALU.bypass, replica_groups=GROUPS,
                                                    ins=[sa], outs=[da]),
             r=list(rkey) if isinstance(rkey, (list, tuple)) else [rkey], w=[wkey], dma=semname, inc=1)

    dbg = {}

    def dbg_out(name, src):
        if debug:
            shape = list(src.shape)
            t = nc.dram_tensor("dbg_" + name, shape, src.dtype, kind="ExternalOutput")
            dbg[name] = (t, src)

    A = Arena(nc, 207 * 1024)
    psb = [nc.alloc_psum_tensor(f"ps{i}", [128, 512], F32) for i in range(8)]

    def ps32(b):
        return psb[b][:, :]

    def ps16(b):
        return psb[b][:, :].bitcast(BF16)

    ident = A.alloc([128], BF16)
    maskT = A.alloc([128], BF16)
    tri = A.alloc([128], F32)
    ones = A.alloc([128], F32)
    striu = A.alloc([32], F32)
    mask64 = A.alloc([64], F32)
    seg = A.alloc([1024], F32)
    onec = A.alloc([1], F32)
    epsc = A.alloc([1], F32)
    hmask = A.alloc([1], F32)
    selc = A.alloc([4], F32)
    gb = [A.alloc([D], F32), A.alloc([D], F32)]
    cw_all = [A.alloc([3, 88], F32) for _ in range(DEPTH)]
    cb_all = [A.alloc([88], F32) for _ in range(DEPTH)]
    S.dma("pool", ident, c_ident[:, :], "ident", w=["ident"])
    S.dma("pool", maskT, c_maskT[:, :], "maskT", w=["maskT"])
    S.dma("sp", tri, c_maskT[:, :], "tri", w=["tri"])
    S.dma("sp", ones, c_ones[:, :], "ones", w=["ones"])
    S.dma("sp", striu[0:32, :], c_striu[:, :], "striu", w=["striu"])
    S.dma("sp", mask64[0:64, :], c_maskT[0:64, 0:64], "mask64", w=["mask64"])
    S.dma("sp", seg, c_seg[:, :], "seg", w=["seg"])
    S.dma("sp", hmask, c_hmask[:, :], "hmask", w=["hmask"])
    S.dma("sp", selc, c_sel[:, :], "selc", w=["selc"])
    identf = A.alloc([128], F32)
    cwT = A.alloc([4, 128], F32)
    S.dma("sp", identf, c_ident[:, :], "identf", w=["identf"])
    for l_ in range(DEPTH):
        S.dma("sp", cwT[0:88, 0:3, :], conv_w[l_].rearrange("j (fb p) -> fb j p", p=128), "cwT", w=["cwT"])
        S.dma("sp", cwT[0:88, 3, :], conv_b[l_].rearrange("(fb p) -> fb p", p=128), "cwT", w=["cwT"])
        for j4 in range(4):
            S.mm(ps32(4)[:, j4 * 88:(j4 + 1) * 88], cwT[0:88, j4, :], identf[0:88, 0:88], r=["cwT", "identf"], w=["ps4"])
        S.copy("dve", cw_all[l_], ps32(4)[:, 0:264].rearrange("p (j f) -> p j f", f=88), r=["ps4"], w=["cw"])
        S.copy("dve", cb_all[l_], ps32(4)[:, 264:352], r=["ps4"], w=["cb"])
    S.memset("dve", onec, 1.0, w=["onec"])
    S.memset("dve", epsc, EPS, w=["epsc"])
    gbi = [0]

    def load_gain(gt, l, n=D, npart=128):
        i = gbi[0] % 2
        gbi[0] += 1
        key = f"gb{i}"
        S.dma("sp", gb[i][0:npart, 0:n], gt[l, :].partition_broadcast(npart), key, w=[key])
        return gb[i], key

    rr = [0]

    def evac_eng():
        rr[0] += 1
        return "act" if rr[0] % 2 else "dve"

    base_mark = A.mark()

    def norm_transpose(src_ap, src_keys, npart, gain, gkey, dstT, dst_key, col0, bufs, pre_scale=None, part=0):
        junk, ss, sd, rstd, hnb, tag = bufs
        if part in (0, 1):
            _norm_stats(src_ap, src_keys, npart, gain, gkey, bufs)
        if part in (0, 2):
            _norm_tr(npart, dstT, dst_key, col0, bufs)

    def _norm_stats(src_ap, src_keys, npart, gain, gkey, bufs):
        junk, ss, sd, rstd, hnb, tag = bufs
        S.act(junk[0:npart, :], src_ap, AF.Square, accum=ss[0:npart, :], r=src_keys, w=[tag + "junk", tag + "ss"])
        S.act(sd[0:npart, :], ss[0:npart, :], AF.Sqrt, bias=epsc[0:npart, :], scale=1.0 / D,
              r=[tag + "ss", "epsc"], w=[tag + "sd"])
        S.op("dve", lambda e: e.reciprocal(rstd[0:npart, :], sd[0:npart, :]), [tag + "sd"], [tag + "rstd"])
        S.stt("dve", hnb[0:npart, :], src_ap, rstd[0:npart, :], gain[0:npart, :], ALU.mult, ALU.mult,
              r=src_keys + [tag + "rstd", gkey], w=[tag + "hnb"])

    def _norm_tr(npart, dstT, dst_key, col0, bufs):
        junk, ss, sd, rstd, hnb, tag = bufs
        for half in range(2):
            bank = 6 + half
            pv = ps16(bank)
            for k8 in range(8):
                kc = half * 8 + k8
                S.tr(pv[:, k8 * 128:k8 * 128 + npart], hnb[0:npart, kc * 128:(kc + 1) * 128], ident[0:npart, 0:npart],
                     r=[tag + "hnb", "ident"], w=[f"ps{bank}"] if k8 in (0, 7) else [])
            src = pv.rearrange("p (a b) -> p a b", b=128)[:, :, 0:npart]
            S.copy(evac_eng(), dstT[:, half * 8:(half + 1) * 8, col0:col0 + npart], src,
                   r=[f"ps{bank}"], w=[dst_key])

    def resid_update(val_ap, val_keys, gain, gkey, hsrc_ap, hdst_aps, bufs, tagkey, alias=()):
        junk, ss, sd, rstd, ht, tag = bufs
        S.dma("sp", ht, hsrc_ap, tag + "ht", r=[tagkey], w=[tag + "ht"] + list(alias))
        S.act(junk, val_ap, AF.Square, accum=ss, r=val_keys, w=[tag + "junk", tag + "ss"])
        S.act(sd, ss, AF.Sqrt, bias=epsc, scale=1.0 / D, r=[tag + "ss", "epsc"], w=[tag + "sd"])
        S.op("dve", lambda e: e.reciprocal(rstd, sd), [tag + "sd"], [tag + "rstd"])
        S.stt("dve", val_ap, val_ap, rstd, gain, ALU.mult, ALU.mult, r=val_keys + [tag + "rstd", gkey], w=val_keys)
        S.tt("dve", ht, ht, val_ap, ALU.add, r=val_keys + [tag + "ht"], w=[tag + "ht"])
        for hd in hdst_aps:
            S.dma("sp", hd, ht, tag + "ht", r=[tag + "ht"], w=[tagkey])

    for l in range(DEPTH):
        hsrc = x if l == 0 else hbuf
        A.reset(base_mark)
        hnT = A.alloc([KC, NTOK], BF16)
        wsl = [A.alloc([KC, 512], BF16) for _ in range(2)]
        ht = [A.alloc([D], F32) for _ in range(2)]
        hnb = [A.alloc([D], BF16) for _ in range(2)]
        junk = A.alloc([D], BF16)
        st_ss = [A.alloc([1], F32) for _ in range(6)]
        stg_fm = [A.alloc([1024], F32) for _ in range(2)]
        stg_tm = [A.alloc([8, 512], BF16) for _ in range(2)]
        stg_fa = A.alloc([8, 8], F32)
        wsm = A.alloc([KC, 8], BF16)
        gain, gkey = load_gain(g_mix_pre, l)
        hk_all = [f"hnT{t_}" for t_ in range(NT)]

        def p1_stats(i, hsrc=hsrc, gain=gain, gkey=gkey):
            b = i % 2
            S.dma("sp", ht[b], hsrc[i * 128:(i + 1) * 128, :], f"p1ht{b}", r=[f"hbuf{i}"], w=[f"p1ht{b}"])
            norm_transpose(ht[b], [f"p1ht{b}"], 128, gain, gkey, hnT, f"hnT{i}", i * 128,
                           (junk, st_ss[0 + b], st_ss[2 + b], st_ss[4 + b], hnb[b], f"p1n{b}"), part=1)

        def p1_tr(i, gain=gain, gkey=gkey):
            b = i % 2
            norm_transpose(ht[b], [f"p1ht{b}"], 128, gain, gkey, hnT, f"hnT{i}", i * 128,
                           (junk, st_ss[0 + b], st_ss[2 + b], st_ss[4 + b], hnb[b], f"p1n{b}"), part=2)

        p1_stats(0)
        p1_stats(1)
        p1_tr(0)
        if upto < 1:
            for i in range(NT):
                if i + 2 < NT:
                    p1_stats(i + 2)
                if i + 1 < NT:
                    p1_tr(i + 1)
        if upto >= 1:
            w_l = w_in[l].rearrange("(kc p) n -> p kc n", p=128)
            blocks = [("tm", 5128, 3, 0, 0), ("fm", 3080, 1, 2, 0), ("fm", 3592, 1, 2, 4),
                      ("ff", 4104, 2, 0, 0), ("ff", 4616, 2, 0, 4),
                      ("tm", 5640, 3, 0, 4), ("tm", 6152, 3, 2, 0), ("tm", 6664, 3, 2, 4),
                      ("fm", 0, 0, 0, 0), ("fm", 512, 0, 0, 4), ("fm", 1024, 0, 2, 0), ("fm", 1536, 0, 2, 4),
                      ("tm", 2048, 1, 0, 0), ("tm", 2560, 1, 0, 4), ("fa", 3072, 4, 0, 0)]
            done_at = {11: 0, 13: 1, 14: 4}
            bi = 0
            sfm = 0
            stm = 0
            pb = 0
            pending = []
            agq = []

            def flush(upto_n):
                while pending and pending[0][0] <= upto_n:
                    pending.pop(0)[1]()

            zs_t = ZS[l].ap().rearrange("(ci s x) (y c) -> ci s (x y) c", ci=20, s=4, c=128).rearrange(
                "ci s (i p) c -> p ci s i c", p=128)
            zs_f = ZS[l].ap().bitcast(F32).rearrange("(ci h p two) c -> p ci h (two c)", ci=20, h=2, two=2)
            zs_fa = ZS[l].ap().bitcast(F32).rearrange("(ci x) c -> ci (x c)", ci=20)
            for n_blk, (kind, c0, kch, slot0, h0) in enumerate(blocks):
                sl = bi % 2
                bi += 1
                wc0 = c0 + 8 - 512 if kind == "fa" else c0
                S.dma("pool", wsl[sl], w_l[:, :, wc0:wc0 + 512], f"wsl{sl}", w=[f"wsl{sl}"])
                flush(n_blk - 2)
                if kind == "fa":
                    for i in range(NT):
                        bank = pb % 4
                        pb += 1
                        for kc in range(KC):
                            S.mm(ps32(bank)[:, 0:8], hnT[:, kc, i * 128:(i + 1) * 128], wsl[sl][:, kc, 504:512],
                                 start=(kc == 0), stop=(kc == KC - 1), r=[f"hnT{i}", f"wsl{sl}"],
                                 w=[f"ps{bank}"] if kc in (0, KC - 1) else [])
                        S.copy(evac_eng(), stg_fa[:, i, :], ps32(bank)[:, 0:8], r=[f"ps{bank}"], w=["stg_fa"])
                    for rp in range(4):
                        S.dma("sp", zs_fa[rp * 5 + 4, 0:2048].rearrange("(i p j) -> p i j", p=128, j=2),
                              stg_fa[:, :, 2 * rp:2 * rp + 2], "stg_fa", r=["stg_fa"], w=["ZS"])
                elif kind in ("fm", "ff"):
                    for fb in range(4):
                        h = h0 + fb
                        ci = (h // 2) * 5 + kch
                        jj = h % 2
                        sb = sfm % 2
                        sfm += 1
                        stg = stg_fm[sb] if kind == "ff" else stg_fm[sb].bitcast(BF16)[:, 0:1024]
                        for half in range(2):
                            bank = pb % 4
                            pb += 1
                            for kc in range(KC):
                                S.mm(ps32(bank), wsl[sl][:, kc, fb * 128:(fb + 1) * 128],
                                     hnT[:, kc, half * 512:(half + 1) * 512],
                                     start=(kc == 0), stop=(kc == KC - 1), r=hk_all + [f"wsl{sl}"],
                                     w=[f"ps{bank}"] if kc in (0, KC - 1) else [])
                            S.copy(evac_eng(), stg[:, half * 512:(half + 1) * 512], ps32(bank),
                                   r=[f"ps{bank}"], w=[f"stg_fm{sb}"])
                        if kind == "ff":
                            dz = zs_f[:, ci, jj, :]
                        else:
                            r0 = ci * 512 + (slot0 + jj) * 128
                            dz = ZS[l][r0:r0 + 128, :]
                        S.dma("sp", dz, stg, f"stg_fm{sb}", r=[f"stg_fm{sb}"], w=["ZS"])
                else:
                    sb = stm % 2
                    stm += 1
                    for i in range(NT):
                        bank = pb % 4
                        pb += 1
                        if n_blk == 0 and i + 2 < NT:
                            p1_stats(i + 2)
                        for kc in range(KC):
                            S.mm(ps32(bank), hnT[:, kc, i * 128:(i + 1) * 128], wsl[sl][:, kc, :],
                                 start=(kc == 0), stop=(kc == KC - 1), r=[f"hnT{i}", f"wsl{sl}"],
                                 w=[f"ps{bank}"] if kc in (0, KC - 1) else [])
                        S.copy(evac_eng(), stg_tm[sb][:, i, :], ps32(bank), r=[f"ps{bank}"], w=[f"stg_tm{sb}"])
                        if n_blk == 0 and i + 1 < NT:
                            p1_tr(i + 1)
                    for w4 in range(4):
                        h = h0 + w4
                        ci = (h // 2) * 5 + kch
                        S.dma("sp", zs_t[:, ci, slot0 + h % 2, :, :], stg_tm[sb][:, :, w4 * 128:(w4 + 1) * 128],
                              f"stg_tm{sb}", r=[f"stg_tm{sb}"], w=["ZS"])
                if n_blk in done_at:
                    kd = done_at[n_blk]
                    for rp in range(4):
                        ci = rp * 5 + kd
                        nrow = {0: 512, 1: 256, 4: 4}[kd]
                        sa_ = ZS[l][ci * 512:ci * 512 + nrow, :]
                        da_ = GZ[l][ci * 2048:ci * 2048 + 4 * nrow, :]
                        if kd == 4:
                            sa_, da_ = sa_.bitcast(F32), da_.bitcast(F32)
                        agq.append((kd, (lambda sa__, da__, kd_: (lambda extra: allgather(
                            sa__, da__, ["ZS"] + extra, f"GZk{kd_}", f"cc_z{kd_}")))(sa_, da_, kd)))
        S.barrier(skip=("cc_",))
        if upto < 1:
            break
        A.reset(base_mark)
        yT = A.alloc([KC, NTOK], BF16)
        yT_mark = A.mark()
        wo = A.alloc([KC, D], BF16)
        wo_mark = A.mark()
        A.reset(yT_mark)
        qtl_all = A.alloc([8, 1024], BF16)
        e2_all = A.alloc([8, 16], F32)
        Sloc = A.alloc([8, 128], F32)
        dtot = A.alloc([8], F32)
        o_all = A.alloc([8, 16, 128], F32)
        oss = A.alloc([16], F32)
        osd = A.alloc([16], F32)
        ors = A.alloc([16], F32)
        emid = A.alloc([16], F32)
        dlast = A.alloc([16], F32)
        dlm = A.alloc([16], F32)
        lsum = A.alloc([16], F32)
        pinc = A.alloc([16], F32)
        ones16 = A.alloc([16], F32)
        Sq = [A.alloc([128], BF16) for _ in range(4)]
        lbw8 = A.alloc([2, 8], F32)
        lbv = A.alloc([8], F32)
        oml = A.alloc([8], F32)
        gho = A.alloc([128], F32)
        hg_mark = A.mark()
        qTb2 = [A.alloc([1024], BF16) for _ in range(3)]
        fTb2 = [A.alloc([1024], F32) for _ in range(3)]
        vch2 = [A.alloc([16, 128], BF16) for _ in range(3)]
        t1 = A.alloc([1024], F32)
        t2 = A.alloc([1024], BF16)
        t3 = A.alloc([1024], F32)
        t4 = A.alloc([1024], F32)
        ktl = A.alloc([1024], BF16)
        kt = A.alloc([16, 128], BF16)
        scT = A.alloc([16, 64], BF16)
        dS_all = A.alloc([16, 128], F32)
        S.memset("dve", ones16, 1.0, w=["ones16"])
        S.dma("sp", gho[0:64, :], g_hgrn_out[l, :].partition_broadcast(64), "gho", w=["gho"])
        if l == 0:
            S.memset("dve", lbv, 0.0, w=["lbv"])
        else:
            for li in range(2):
                S.dma("sp", lbw8[:, li, :], w_hgrn_lb[li, :].rearrange("(h p) -> p h", p=128), "lbw", w=["lbw"], slow=True)
            S.tt("dve", lbv, lbw8[:, 1, :], lbw8[:, 0, :], ALU.subtract, r=["lbw"], w=["lbv"])
            S.act(lbv, lbv, AF.Sigmoid, r=["lbv"], w=["lbv"])
        S.ts("dve", oml, lbv, -1.0, 1.0, ALU.mult, ALU.add, r=["lbv"], w=["oml"])
        S.memset("dve", Sloc, 0.0, w=["Sloc"])
        zs_n = ZS[l].ap().rearrange("(ci s x) (y c) -> ci s (x y) c", ci=20, s=4, c=128).rearrange(
            "ci s (n p) c -> p ci s n c", p=64)
        zs_ff = ZS[l].ap().bitcast(F32).rearrange("(ci h p two) c -> p ci h (two c)", ci=20, h=2, two=2)
        oh_v = OH.ap().rearrange("(h n p) c -> p h n c", h=8, p=64)
        t1v = t1.rearrange("p (a b) -> p a b", b=64)
        t3v = t3.rearrange("p (a b) -> p a b", b=64)
        def hg_loads(h):
            rp, j = h // 2, h % 2
            b2 = h % 3
            r0 = (rp * 5 + 1) * 512 + (2 + j) * 128
            S.dma("sp", qTb2[b2], ZS[l][r0:r0 + 128, :], f"qTb{b2}", r=["ZS"], w=[f"qTb{b2}"])
            S.dma("sp", fTb2[b2], zs_ff[:, rp * 5 + 2, j, :], f"fTb{b2}", r=["ZS"], w=[f"fTb{b2}"])
            S.dma("sp", vch2[b2][0:64], zs_n[:, rp * 5 + 3, j, :, :], f"vch{b2}", r=["ZS"], w=[f"vch{b2}"])

        agq.sort(key=lambda t_: {4: 0, 0: 1, 1: 2}[t_[0]])
        ag_plan = {2: 5, 3: 1, 4: 1, 5: 1}
        hg_loads(0)
        hg_loads(1)
        for h in range(8):
            rp, j = h // 2, h % 2
            b2 = h % 3
            if h + 2 < 8:
                hg_loads(h + 2)
                nb2 = (h + 2) % 3
                for _ in range(ag_plan.get(h + 2, 0)):
                    if agq:
                        agq.pop(0)[1]([f"qTb{nb2}", f"fTb{nb2}", f"vch{nb2}"])
            qTb, fTb, vch = qTb2[b2], fTb2[b2], vch2[b2]
            kq, kf, kv = f"qTb{b2}", f"fTb{b2}", f"vch{b2}"
            qtl = qtl_all[:, h, :]
            S.act(t1, fTb, AF.Sigmoid, r=[kf], w=["t1"])
            S.ts("dve", t1, t1, oml[:, h:h + 1], lbv[:, h:h + 1], ALU.mult, ALU.add, r=["t1", "oml", "lbv"], w=["t1"])
            S.ts("dve", t2, t1, -1.0, 1.0, ALU.mult, ALU.add, r=["t1"], w=["t2"])
            S.act(t1, t1, AF.Ln, r=["t1"], w=["t1"])
            S.op("dve", lambda e: e.tensor_tensor_scan(t3, seg, t1, 0.0, ALU.mult, ALU.add), ["seg", "t1"], ["t3"])
            S.tt("dve", t1v, t3v, t3v[:, :, 31:32].broadcast_to([128, 16, 64]), ALU.subtract, r=["t3"], w=["t1"])
            S.act(t4, t1, AF.Exp, r=["t1"], w=["t4"])
            S.act(t1, t1, AF.Exp, scale=-1.0, r=["t1"], w=["t1"])
            S.tt("dve", qtl, qTb, t4, ALU.mult, r=[kq, "t4"], w=["qtl_all"])
            S.tt("dve", ktl, t2, t1, ALU.mult, r=["t2", "t1"], w=["ktl"])
            S.act(emid, t3v[:, :, 31], AF.Exp, r=["t3"], w=["emid"])
            S.act(dlast, t3v[:, :, 63], AF.Exp, r=["t3"], w=["dlast"])
            S.tt("dve", dlm, t3v[:, :, 63], t3v[:, :, 31], ALU.subtract, r=["t3"], w=["dlm"])
            S.act(dlm, dlm, AF.Exp, r=["dlm"], w=["dlm"])
            S.copy("dve", lsum, t3v[:, :, 63], r=["t3"], w=["lsum"])
            S.op("dve", lambda e: e.tensor_tensor_scan(pinc, ones16, lsum, 0.0, ALU.mult, ALU.add), ["ones16", "lsum"], ["pinc"])
            S.act(dtot[:, h:h + 1], pinc[:, 15:16], AF.Exp, r=["pinc"], w=["dtot"])
            S.tt("dve", lsum, pinc, lsum, ALU.subtract, r=["pinc", "lsum"], w=["lsum"])
            S.tt("dve", lsum, lsum, t3v[:, :, 31], ALU.add, r=["lsum", "t3"], w=["lsum"])
            S.act(e2_all[:, h, :], lsum, AF.Exp, r=["lsum"], w=["e2_all"])
            for half in range(2):
                bank = 6 + half
                pv = ps16(bank)
                for n8 in range(8):
                    n = half * 8 + n8
                    S.tr(pv[0:64, n8 * 128:(n8 + 1) * 128], ktl[:, n * 64:(n + 1) * 64], ident,
                         r=["ktl", "ident"], w=[f"ps{bank}"] if n8 in (0, 7) else [])
                S.copy(evac_eng(), kt[0:64, half * 8:(half + 1) * 8, :],
                       pv[0:64, :].rearrange("p (a b) -> p a b", b=128), r=[f"ps{bank}"], w=["kt"])
            for half in range(2):
                bank = 4 + half
                for n8 in range(8):
                    n = half * 8 + n8
                    S.mm(ps32(bank)[0:64, n8 * 64:(n8 + 1) * 64], ktl[:, n * 64:(n + 1) * 64],
                         qtl[:, n * 64:(n + 1) * 64], r=["ktl", "qtl_all"], w=[f"ps{bank}"] if n8 in (0, 7) else [])
                S.tt("dve", scT[0:64, half * 8:(half + 1) * 8, :],
                     ps32(bank)[0:64, :].rearrange("p (a b) -> p a b", b=64),
                     mask64[0:64, :].rearrange("p (o b) -> p o b", o=1).broadcast_to([64, 8, 64]), ALU.mult,
                     r=[f"ps{bank}", "mask64"], w=["scT"])
            osb = o_all[:, h, :, :]
            okey = "o_all"
            Sh = Sloc[:, h, :]
            for g4 in range(4):
                bank = g4 % 2 + 2
                for n4 in range(4):
                    n = g4 * 4 + n4
                    S.mm(ps32(bank)[:, n4 * 128:(n4 + 1) * 128], kt[0:64, n, :], vch[0:64, n, :], r=["kt", kv],
                         w=[f"ps{bank}"] if n4 in (0, 3) else [])
                S.tt("dve", dS_all[:, g4 * 4:(g4 + 1) * 4, :], ps32(bank).rearrange("p (a b) -> p a b", b=128),
                     dlm[:, g4 * 4:(g4 + 1) * 4].rearrange("p (a o) -> p a o", o=1).broadcast_to([128, 4, 128]), ALU.mult,
                     r=[f"ps{bank}", "dlm"], w=["dS_all"])
            for n in range(16):
                sq = Sq[n % 4]
                ob = n % 2
                S.ts("dve", sq, Sh, emid[:, n:n + 1], None, ALU.mult, r=["Sloc", "emid"], w=[f"Sq{n % 4}"])
                S.stt("dve", Sh, Sh, dlast[:, n:n + 1], dS_all[:, n, :], ALU.mult, ALU.add,
                      r=["Sloc", "dlast", "dS_all"], w=["Sloc"])
                S.mm(ps32(ob)[0:64, 0:128], qtl[:, n * 64:(n + 1) * 64], sq, start=True, stop=False,
                     r=["qtl_all", f"Sq{n % 4}"], w=[f"ps{ob}"])
                S.mm(ps32(ob)[0:64, 0:128], scT[0:64, n, :], vch[0:64, n, :], start=False, stop=True,
                     r=["scT", kv], w=[f"ps{ob}"])
                S.copy("act", osb[0:64, n, :], ps32(ob)[0:64, 0:128], r=[f"ps{ob}"], w=[okey])
        S.dma("sp", ST[0:1024, :].rearrange("(h k) v -> k h v", h=8), Sloc, "Sloc", r=["Sloc"], w=["ST"])
        S.dma("sp", ST[1024:1032, :].rearrange("h k -> k h"), dtot, "dtot", r=["dtot"], w=["ST"], slow=True)
        allgather(ST.ap(), GST.ap(), "ST", "GST", "cc_s")
        while agq:
            agq.pop(0)[1]([])
        S.barrier(skip=("cc_",))
        A.reset(hg_mark)
        gch2 = [A.alloc([16, 128], BF16) for _ in range(2)]
        osq = A.alloc([16, 128], F32)
        sgl = A.alloc([16, 128], F32)
        ybf = A.alloc([16, 128], BF16)
        gst = A.alloc([4, 8, 128], F32)
        gdt = A.alloc([4, 8], F32)
        Acc = A.alloc([8, 128], F32)
        Sin = A.alloc([8, 128], F32)
        Sinb = A.alloc([8, 128], BF16)
        q2 = [A.alloc([1024], BF16) for _ in range(2)]
        for h in range(2):
            S.dma("sp", gch2[h % 2][0:64], zs_n[:, (h // 2) * 5 + 3, 2 + h % 2, :, :], f"gch{h % 2}", r=["ZS"], w=[f"gch{h % 2}"])
        gst_v = GST.ap().rearrange("(rho x) v -> rho x v", rho=4)
        for rho in range(4):
            S.dma("sp", gst[:, rho, :, :], gst_v[rho, 0:1024, :].rearrange("(h k) v -> k h v", h=8), "gst", r=["GST"], w=["gst"])
            S.dma("sp", gdt[:, rho, :], gst_v[rho, 1024:1032, :].rearrange("h k -> k h"), "gdt", r=["GST"], w=["gdt"], slow=True)
        S.memset("dve", Acc, 0.0, w=["Acc"])
        S.memset("dve", Sin, 0.0, w=["Sin"])
        for rho in range(4):
            S.stt("dve", Sin, Acc, selc[:, rho:rho + 1], Sin, ALU.mult, ALU.add, r=["Acc", "selc", "Sin"], w=["Sin"])
            if rho < 3:
                S.tt("dve", Acc, Acc, gdt[:, rho, :].rearrange("p (h o) -> p h o", o=1).broadcast_to([128, 8, 128]), ALU.mult,
                     r=["Acc", "gdt"], w=["Acc"])
                S.tt("dve", Acc, Acc, gst[:, rho, :, :], ALU.add, r=["Acc", "gst"], w=["Acc"])
        S.copy("dve", Sinb, Sin, r=["Sin"], w=["Sinb"])
        def p2_stage_a(h):
            rp, j = h // 2, h % 2
            osb = o_all[:, h, :, :]
            okey = f"o_all{h}"
            qq = q2[h % 2]
            S.tt("dve", qq.rearrange("p (n t) -> p n t", t=64), qtl_all[:, h, :].rearrange("p (n t) -> p n t", t=64),
                 e2_all[:, h, :].rearrange("p (n o) -> p n o", o=1).broadcast_to([128, 16, 64]), ALU.mult,
                 r=["qtl_all", "e2_all"], w=[f"q2{h % 2}"])
            for g4 in range(4):
                for n4 in range(4):
                    n = g4 * 4 + n4
                    S.mm(ps32(g4)[0:64, n4 * 128:(n4 + 1) * 128], qq[:, n * 64:(n + 1) * 64], Sinb[:, h, :],
                         r=[f"q2{h % 2}", "Sinb"], w=[f"ps{g4}"] if n4 in (0, 3) else [])
                S.tt("dve", osb[0:64, g4 * 4:(g4 + 1) * 4, :], osb[0:64, g4 * 4:(g4 + 1) * 4, :],
                     ps32(g4)[0:64, :].rearrange("p (a b) -> p a b", b=128), ALU.add, r=[okey, f"ps{g4}"], w=[okey])

        def p2_stage_b(h):
            rp, j = h // 2, h % 2
            osb = o_all[:, h, :, :]
            okey = f"o_all{h}"
            gch = gch2[h % 2]
            S.tt("pool", osq[0:64], osb[0:64], osb[0:64], ALU.mult, r=[okey], w=["osq"])
            S.op("dve", lambda e: e.tensor_reduce(oss[0:64, :], osq[0:64], AX.X, ALU.add), ["osq"], ["oss"])
            S.act(osd[0:64, :], oss[0:64, :], AF.Sqrt, bias=epsc[0:64, :], scale=1.0 / 128, r=["oss", "epsc"], w=["osd"])
            S.op("dve", lambda e: e.reciprocal(ors[0:64, :], osd[0:64, :]), ["osd"], ["ors"])
            S.act(sgl[0:64], gch[0:64], AF.Silu, r=[f"gch{h % 2}"], w=["sgl"])
            S.tt("pool", sgl[0:64], sgl[0:64],
                 gho[0:64, :].rearrange("p (o b) -> p o b", o=1).broadcast_to([64, 16, 128]), ALU.mult,
                 r=["sgl", "gho"], w=["sgl"])
            S.tt("dve", osq[0:64], osb[0:64],
                 ors[0:64, :].rearrange("p (a o) -> p a o", o=1).broadcast_to([64, 16, 128]), ALU.mult,
                 r=[okey, "ors"], w=["osq"])
            if h + 2 < 8:
                h2 = h + 2
                S.dma("sp", gch2[h2 % 2][0:64], zs_n[:, (h2 // 2) * 5 + 3, 2 + h2 % 2, :, :], f"gch{h2 % 2}", r=["ZS"], w=[f"gch{h2 % 2}"])
            S.tt("dve", ybf[0:64], osq[0:64], sgl[0:64], ALU.mult, r=["osq", "sgl"], w=["ybf"])
            for half in range(2):
                bank = 6 + half
                pv = ps16(bank)
                for n8 in range(8):
                    n = half * 8 + n8
                    S.tr(pv[:, n8 * 64:(n8 + 1) * 64], ybf[0:64, n, :], ident[0:64, 0:64],
                         r=["ybf", "ident"], w=[f"ps{bank}"] if n8 in (0, 7) else [])
                S.copy(evac_eng(), yT[:, 8 + h, half * 512:(half + 1) * 512], pv[:, 0:512], r=[f"ps{bank}"], w=["yT"])

        p2_stage_a(0)
        for h in range(8):
            if h + 1 < 8:
                p2_stage_a(h + 1)
            p2_stage_b(h)
        S.barrier(skip=("cc_",))
        if upto < 2:
            break
        A.reset(wo_mark)
        ystage = A.alloc([32, 256], BF16)
        QK = A.alloc([16, 1024], BF16)
        VA = A.alloc([64, 128], BF16)
        V1 = A.alloc([32, 129], BF16)
        fl2 = A.alloc([32, 2], F32)
        bsel = A.alloc([2], F32)
        nbs = A.alloc([2], F32)
        pos = A.alloc([32], F32)
        e1 = A.alloc([32], F32)
        csT = A.alloc([128], F32)
        Cs = A.alloc([32], F32)
        offs = A.alloc([32], F32)
        bIJ = A.alloc([32, 32], F32)
        PT = [A.alloc([128], BF16) for _ in range(6)]
        rec = [A.alloc([1], F32) for _ in range(2)]
        gz_x = GZ[l].ap().rearrange("(rp k x p) t -> p rp k x t", rp=4, k=5, x=16, p=128)
        gz_tok = GZ[l].ap().rearrange("(rp k r2) (y c) -> rp k (r2 y) c", rp=4, k=5, c=128)
        gz_g128 = gz_tok.rearrange("rp k (g p) c -> p rp k g c", p=128)
        gz_fa = GZ[l].ap().bitcast(F32).rearrange("(rp k x) c -> rp k (x c)", rp=4, k=5)
        S.dma("sp", QK, lambda rv: gz_x[:, bass.ds(rv[0], 1), 0, :, :].rearrange("p a x t -> p (a x) t"),
              "QK", r=["GZk0"], w=["QK"])
        S.dma("sp", VA, lambda rv: gz_g128[:, bass.ds(rv[0], 1), 1, 0:64, :].rearrange("p a g c -> p (a g) c"),
              "VA", r=["GZk1"], w=["VA"])
        S.dma("sp", fl2, lambda rv: gz_fa[bass.ds(rv[0], 1), 4, 0:8192].rearrange("a (g p j) -> p (a g) j", p=128, j=2),
              "fl2", r=["GZk4"], w=["fl2"])
        S.dma("sp", bsel, (lambda l_: (lambda rv: b_fox_f[l_, :].partition_broadcast(128)[:, bass.ds(2 * rv[0], 2)]))(l),
              "bsel", w=["bsel"])
        wo_v = w_out[l].rearrange("(kc p) n -> p kc n", p=128)
        for nb in range(4):
            S.dma("pool", wo[:, :, nb * 512:(nb + 1) * 512], wo_v[:, :, nb * 512:(nb + 1) * 512], "wo",
                  r=["QK", "VA", "fl2"], w=["wo"])
        S.memset("dve", V1[:, :, 128:129], 1.0, w=["V1"])
        VA5 = VA.rearrange("p (rho s i) c -> p rho s i c", rho=4, s=2)
        S.ts("dve", nbs, bsel, -1.0, None, ALU.mult, r=["bsel"], w=["nbs"])
        for j in range(2):
            S.copy("pool", V1[:, :, 0:128].rearrange("p (rho i) c -> p rho i c", rho=4), VA5[:, :, j, :, :],
                   r=["VA"], w=["V1"])
            S.act(e1, fl2[:, :, j], AF.Exp, bias=nbs[:, j:j + 1], scale=-1.0, r=["fl2", "nbs"], w=["e1"])
            S.act(pos, e1, AF.Ln, bias=onec, scale=1.0, r=["e1", "onec"], w=["pos"])
            S.mm(ps32(4)[0:32, 0:128], pos, ones, r=["pos", "ones"], w=["ps4"])
            S.copy("dve", csT[0:32, :], ps32(4)[0:32, 0:128], r=["ps4"], w=["csT"])
            S.mm(ps32(5)[:, 0:32], tri, pos, start=True, stop=False, r=["tri", "pos"], w=["ps5"])
            S.mm(ps32(5)[:, 0:32], csT[0:32, :], striu[0:32, :], start=False, stop=True, r=["csT", "striu"], w=["ps5"])
            S.mm(ps32(5)[:, 64:96], csT[0:32, :], striu[0:32, :], start=True, stop=True, r=["csT", "striu"], w=["ps5"])
            S.copy("dve", Cs, ps32(5)[:, 0:32], r=["ps5"], w=["Cs"])
            S.copy("dve", offs, ps32(5)[:, 64:96], r=["ps5"], w=["offs"])
            for I in range(32):
                S.ts("dve", bIJ[:, I, :], Cs, offs[:, I:I + 1], None, ALU.subtract, r=["Cs", "offs"], w=["bIJ"])
            pairs = [(I, J) for I in range(32) for J in range(I + 1)]
            LA = 3
            STB = [0, 1, 4, 5]
            NPT = len(PT)
            for n in range(len(pairs) + LA):
                if n < len(pairs):
                    I, J = pairs[n]
                    sbk = STB[n % 4]
                    pt = n % NPT
                    qsl = QK[:, (I // 8) * 4 + j, (I % 8) * 128:(I % 8 + 1) * 128]
                    ksl = QK[:, (J // 8) * 4 + 2 + j, (J % 8) * 128:(J % 8 + 1) * 128]
                    S.mm(ps32(sbk)[:, 0:128], ksl, qsl, r=["QK"], w=[f"ps{sbk}"])
                    S.act(PT[pt], ps32(sbk)[:, 0:128], AF.Exp, bias=bIJ[:, I, J:J + 1], scale=SCALE,
                          r=[f"ps{sbk}", "bIJ"], w=[f"PT{pt}"])
                    if J == I:
                        S.tt("dve", PT[pt], PT[pt], maskT, ALU.mult, r=[f"PT{pt}", "maskT"], w=[f"PT{pt}"])
                m_ = n - LA
                if m_ >= 0:
                    I, J = pairs[m_]
                    pt = m_ % NPT
                    ob = 2 + I % 2
                    S.mm(ps32(ob)[:, 0:129], PT[pt], V1[:, J, :], start=(J == 0), stop=(J == I),
                         r=[f"PT{pt}", "V1"], w=[f"ps{ob}"] if J in (0, I) else [])
                    if J == I:
                        rc = rec[I % 2]
                        S.op("dve", (lambda rc_, ob_: (lambda e: e.reciprocal(rc_, ps32(ob_)[:, 128:129])))(rc, ob),
                             [f"ps{ob}"], [f"rec{I % 2}"])
                        hf = I // 8
                        S.ts("dve", ystage[:, I, j * 128:(j + 1) * 128], ps32(ob)[:, 0:128], rc, None, ALU.mult,
                             r=[f"ps{ob}", f"rec{I % 2}"], w=[f"ystage{j}{hf}"])
                        if I % 8 == 7:
                            S.dma("sp", ysA[j].ap().rearrange("(J p) c -> p J c", p=128)[:, hf * 8:(hf + 1) * 8, :],
                                  ystage[:, hf * 8:(hf + 1) * 8, j * 128:(j + 1) * 128], f"ystage{j}{hf}",
                                  r=[f"ystage{j}{hf}"], w=[f"ysA{j}{hf}"])
                            allgather(ysA[j][hf * 1024:(hf + 1) * 1024, :], GYA[j][hf * 81920:hf * 81920 + 4096, :],
                                      f"ysA{j}{hf}", f"GYA{j}", "cc_ya")
        S.barrier(skip=("cc_",))
        if upto < 3:
            break
        A.reset(base_mark)
        yT = A.alloc([KC, NTOK], BF16)
        wo = A.alloc([KC, D], BF16)
        YTA = A.alloc([32, 256], BF16)
        mixsb = [A.alloc([D], F32) for _ in range(2)]
        hts = [A.alloc([D], F32) for _ in range(2)]
        junk = A.alloc([D], BF16)
        st_ss = [A.alloc([1], F32) for _ in range(6)]
        for jj_ in range(2):
            gya_v = GYA[jj_].ap().rearrange("(rp g p) c -> p rp g c", rp=4, p=128)
            S.dma("sp", YTA[:, :, jj_ * 128:(jj_ + 1) * 128],
                  (lambda V_: (lambda rv: V_[:, bass.ds(rv[0], 1), 0:32, :].rearrange("p a g c -> p (a g) c")))(gya_v),
                  "YTA", r=[f"GYA{jj_}"], w=["YTA"])
        for i in range(NT):
            pv = ps16(6 + i % 2)
            bank = 6 + i % 2
            for k8 in range(8):
                m, jj = k8 // 2, k8 % 2
                S.tr(pv[:, k8 * 128:(k8 + 1) * 128], YTA[:, m * 8 + i, jj * 128:(jj + 1) * 128], ident,
                     r=["YTA", "ident"], w=[f"ps{bank}"] if k8 in (0, 7) else [])
            S.copy(evac_eng(), yT[:, 0:8, i * 128:(i + 1) * 128], pv.rearrange("p (a b) -> p a b", b=128),
                   r=[f"ps{bank}"], w=["yT"])
        gain, gkey = load_gain(g_mix_post, l)
        gain2, gkey2 = load_gain(g_ffn_pre, l)
        hnq = [A.alloc([D], BF16) for _ in range(2)]
        st_q = [A.alloc([1], F32) for _ in range(6)]
        junkq = A.alloc([D], BF16)
        for i in range(NT):
            b = i % 2
            for nb in range(4):
                bank = nb + 4 * (i % 2)
                for kc in range(KC):
                    S.mm(ps32(bank), yT[:, kc, i * 128:(i + 1) * 128], wo[:, kc, nb * 512:(nb + 1) * 512],
                         start=(kc == 0), stop=(kc == KC - 1), r=["yT", "wo"], w=[f"ps{bank}"] if kc in (0, KC - 1) else [])
                S.copy(evac_eng(), mixsb[b][:, nb * 512:(nb + 1) * 512], ps32(bank), r=[f"ps{bank}"], w=[f"mixsb{b}"])
            dsts = [hbuf[i * 128:(i + 1) * 128, :]]
            resid_update(mixsb[b], [f"mixsb{b}"], gain, gkey, hsrc[i * 128:(i + 1) * 128, :], dsts,
                         (junk, st_ss[0 + b], st_ss[2 + b], st_ss[4 + b], hts[b], f"p3a{b}"), f"hbuf{i}")
            if i == NT - 1:
                S.dma("sp", hh[l][:, :], hts[b][126:128, :], f"p3a{b}ht", r=[f"p3a{b}ht"], w=["hh"])
            _norm_stats(hts[b], [f"p3a{b}ht"], 128, gain2, gkey2,
                        (junkq, st_q[0 + b], st_q[2 + b], st_q[4 + b], hnq[b], f"p3aq{b}"))
            S.dma("sp", hn2buf[i * 128:(i + 1) * 128, :], hnq[b], f"p3aq{b}hnb", r=[f"p3aq{b}hnb"], w=[f"hn2buf{i}"])
        allgather(hh[l].ap(), ghh[l].ap(), "hh", "ghh", "cc_h")
        if debug and l == 0 and upto == 3:
            dbg_out("hbuf_a", hbuf.ap())
        S.barrier()
        if upto < 4:
            break
        A.reset(base_mark)
        hn2T = A.alloc([KC, NTOK + 2], BF16)
        a_seg = A.alloc([NFB, 512], BF16)
        cw = cw_all[l]
        cb = cb_all[l]
        uhalo = A.alloc([88, 2], F32)
        junk = A.alloc([D], BF16)
        st_ss = [A.alloc([1], F32) for _ in range(6)]
        hn2t2 = A.alloc([2, D], BF16)
        hn2t = [hn2t2[:, 0, :], hn2t2[:, 1, :]]
        fm = A.mark()
        gain, gkey = load_gain(g_ffn_pre, l)
        for i in range(NT):
            b = i % 2
            S.dma("sp", hn2t[b], hn2buf[i * 128:(i + 1) * 128, :], f"p3bn{b}hnb", r=[f"hn2buf{i}"], w=[f"p3bn{b}hnb"])
            _norm_tr(128, hn2T, "hn2T", 2 + i * 128, (junk, None, None, None, hn2t[b], f"p3bn{b}"))
        a_flat = a_seg.rearrange("p a b -> p (a b)")
        hl_f = a_flat[:, 0:4096].bitcast(F32)
        hl_b = a_flat[:, 4096:6144]
        hl_j = a_flat[:, 6144:8192]
        S.dma("sp", hl_f[0:2, :], (lambda l_: (lambda rv: ghh[l_][bass.ds(((rv[0] + 3) % 4) * 2, 2), :]))(l), "p3bhalo", r=["ghh"], w=["a_seg"])
        S.ts("dve", hl_f[0:2, :], hl_f[0:2, :], hmask[0:2, :], None, ALU.mult, r=["a_seg", "hmask"], w=["a_seg"])
        S.act(hl_j[0:2, :], hl_f[0:2, :], AF.Square, accum=st_ss[0][0:2, :], r=["a_seg"], w=["a_seg", "p3bh_ss"])
        S.act(st_ss[2][0:2, :], st_ss[0][0:2, :], AF.Sqrt, bias=epsc[0:2, :], scale=1.0 / D, r=["p3bh_ss", "epsc"], w=["p3bh_sd"])
        S.op("dve", (lambda a_, b_: (lambda e: e.reciprocal(a_, b_)))(st_ss[4][0:2, :], st_ss[2][0:2, :]),
             ["p3bh_sd"], ["p3bh_rs"])
        S.stt("dve", hl_b[0:2, :], hl_f[0:2, :], st_ss[4][0:2, :], gain[0:2, :], ALU.mult, ALU.mult,
              r=["a_seg", "p3bh_rs", gkey], w=["a_seg"])
        for half in range(2):
            bank = 6 + half
            pv = ps16(bank)
            for k8 in range(8):
                kc = half * 8 + k8
                S.tr(pv[:, k8 * 128:k8 * 128 + 2], hl_b[0:2, kc * 128:(kc + 1) * 128], ident[0:2, 0:2],
                     r=["a_seg", "ident"], w=[f"ps{bank}"] if k8 in (0, 7) else [])
            S.copy(evac_eng(), hn2T[:, half * 8:(half + 1) * 8, 0:2], pv.rearrange("p (a b) -> p a b", b=128)[:, :, 0:2],
                   r=[f"ps{bank}"], w=["hn2T"])
        gainp, gkeyp = load_gain(g_ffn_post, l)
        wu_v = w_up[l].rearrange("(kc p) n -> p kc n", p=128)
        wd_v = w_down[l].rearrange("(fc p) n -> p fc n", p=128)
        for sgm in range(2):
            A.reset(fm)
            wus = [A.alloc([KC, 1024], BF16) for _ in range(2)]
            ubuf = [A.alloc([514], F32) for _ in range(4)]
            cbuf = [A.alloc([512], F32) for _ in range(4)]
            pbk = 0
            for gq in range(11):
                sl = gq % 2
                if not (sgm == 1 and gq == 0):
                    S.dma("pool", wus[sl][:, :, 0:512], wu_v[:, :, gq * 512:(gq + 1) * 512], f"wus{sl}", w=[f"wus{sl}"])
                    S.dma("pool", wus[sl][:, :, 512:1024], wu_v[:, :, D_FF + gq * 512:D_FF + (gq + 1) * 512], f"wus{sl}", w=[f"wus{sl}"])
                for jj in range(4):
                    fbg = gq * 4 + jj
                    cbs = []
                    for which in range(2):
                        fb88 = fbg + which * NFB
                        wc = which * 512 + jj * 128
                        ub = ubuf[(2 * fbg + which) % 4]
                        cbf = cbuf[(2 * fbg + which) % 4]
                        uk = f"ubuf{(2 * fbg + which) % 4}"
                        ck = f"cbuf{(2 * fbg + which) % 4}"
                        bank = pbk % 4
                        pbk += 1
                        if sgm == 0:
                            for kc in range(KC):
                                S.mm(ps32(4 + bank % 2)[:, 0:2], wus[sl][:, kc, wc:wc + 128], hn2T[:, kc, 0:2],
                                     start=(kc == 0), stop=(kc == KC - 1), r=[f"wus{sl}", "hn2T"],
                                     w=[f"ps{4 + bank % 2}"] if kc in (0, KC - 1) else [])
                            S.copy("act", ub[:, 0:2], ps32(4 + bank % 2)[:, 0:2], r=[f"ps{4 + bank % 2}"], w=[uk])
                        else:
                            S.copy("dve", ub[:, 0:2], uhalo[:, fb88, :], r=["uhalo"], w=[uk])
                        for kc in range(KC):
                            S.mm(ps32(bank), wus[sl][:, kc, wc:wc + 128], hn2T[:, kc, 2 + sgm * 512:2 + (sgm + 1) * 512],
                                 start=(kc == 0), stop=(kc == KC - 1), r=[f"wus{sl}", "hn2T"],
                                 w=[f"ps{bank}"] if kc in (0, KC - 1) else [])
                        S.copy("act", ub[:, 2:514], ps32(bank), r=[f"ps{bank}"], w=[uk])
                        S.act(cbf, ps32(bank), AF.Identity, bias=cb[:, fb88:fb88 + 1], scale=cw[:, 2, fb88:fb88 + 1],
                              r=[f"ps{bank}", "cw", "cb"], w=[ck])
                        S.stt("dve", cbf, ub[:, 1:513], cw[:, 1, fb88:fb88 + 1], cbf, ALU.mult, ALU.add, r=[uk, ck, "cw"], w=[ck])
                        S.stt("dve", cbf, ub[:, 0:512], cw[:, 0, fb88:fb88 + 1], cbf, ALU.mult, ALU.add, r=[uk, ck, "cw"], w=[ck])
                        if sgm == 0:
                            S.copy("dve", uhalo[:, fb88, :], ub[:, 512:514], r=[uk], w=["uhalo"])
                        cbs.append((cbf, ck))
                    (cg, cgk), (cu, cuk) = cbs
                    S.act(cg, cg, AF.Silu, r=[cgk], w=[cgk])
                    S.tt("dve", a_seg[:, fbg, :], cg, cu, ALU.mult, r=[cgk, cuk], w=["a_seg"])
            S.barrier()
            A.reset(fm)
            wds = [A.alloc([22, 512], BF16) for _ in range(2)]
            ffsb = A.alloc([4, D], F32)
            hts = [hn2t2.rearrange("p a b -> p (a b)").bitcast(F32), A.alloc([D], F32)]
            for nb in range(4):
                for half in range(2):
                    S.dma("pool", wds[half], wd_v[:, half * 22:(half + 1) * 22, nb * 512:(nb + 1) * 512], f"wds{half}", w=[f"wds{half}"])
                    for t4 in range(4):
                        dbk = t4 + 4 * (nb % 2)
                        for fcl in range(22):
                            fc = half * 22 + fcl
                            S.mm(ps32(dbk), a_seg[:, fc, t4 * 128:(t4 + 1) * 128], wds[half][:, fcl, :],
                                 start=(fc == 0), stop=(fc == NFB - 1), r=["a_seg", f"wds{half}"],
                                 w=[f"ps{dbk}"] if fc in (0, NFB - 1) else [])
                for t4 in range(4):
                    dbk = t4 + 4 * (nb % 2)
                    S.copy(evac_eng(), ffsb[:, t4, nb * 512:(nb + 1) * 512], ps32(dbk), r=[f"ps{dbk}"], w=[f"ffsb{t4}"])
            if sgm == 0:
                S.dma("pool", wus[0][:, :, 0:512], wu_v[:, :, 0:512], "wus0", w=["wus0", "wds0", "wds1"])
                S.dma("pool", wus[0][:, :, 512:1024], wu_v[:, :, D_FF:D_FF + 512], "wus0", w=["wus0", "wds0", "wds1"])
            for t4 in range(4):
                i = sgm * 4 + t4
                b = t4 % 2
                resid_update(ffsb[:, t4, :], [f"ffsb{t4}"], gainp, gkeyp, hbuf[i * 128:(i + 1) * 128, :],
                             [hbuf[i * 128:(i + 1) * 128, :]],
                             (junk, st_ss[0 + b], st_ss[2 + b], st_ss[4 + b], hts[b], f"p3d{b}"), f"hbuf{i}",
                             alias=["p3bn0hnb", "p3bn1hnb"] if b == 0 else [])
            S.barrier()
        if debug and l == 0 and upto == 4:
            dbg_out("hbuf_b", hbuf.ap())
        if upto < 5:
            break
        A.reset(base_mark)
        hn3T = A.alloc([KC, NTOK], BF16)
        pT = A.alloc([2, NTOK], BF16)
        wg = A.alloc([KC, D], BF16)
        wp = A.alloc([2, D], BF16)
        junk = A.alloc([D], BF16)
        junk2 = A.alloc([D], BF16)
        st_ss = [A.alloc([1], F32) for _ in range(6)]
        st_s2 = [A.alloc([1], F32) for _ in range(6)]
        ht = [A.alloc([D], F32) for _ in range(2)]
        hnb = [A.alloc([D], BF16) for _ in range(2)]
        ptk = [A.alloc([256], F32) for _ in range(2)]
        pbf = [A.alloc([256], BF16) for _ in range(2)]
        sgs = [A.alloc([512], F32) for _ in range(2)]
        egs = [A.alloc([D], F32) for _ in range(2)]
        wg_v = w_ple_gate[l].rearrange("(kc p) n -> p kc n", p=128)
        wp_v = w_ple_proj[l].rearrange("(kc p) n -> p kc n", p=128)
        for nb in range(4):
            S.dma("pool", wg[:, :, nb * 512:(nb + 1) * 512], wg_v[:, :, nb * 512:(nb + 1) * 512], "wg", w=["wg"])
        S.dma("pool", wp, wp_v, "wp", w=["wp"])
        gain, gkey = load_gain(g_ple_in, l)
        gainp, gkeyp = load_gain(g_ple_post, l)
        def ple_front(i):
            b = i % 2
            S.dma("sp", ht[b], hbuf[i * 128:(i + 1) * 128, :], f"p3cht{b}", r=[f"hbuf{i}"], w=[f"p3cht{b}"])
            S.dma("sp", ptk[b], p_in[l, i * 128:(i + 1) * 128, :], f"ptk{b}", w=[f"ptk{b}"])
            norm_transpose(ht[b], [f"p3cht{b}"], 128, gain, gkey, hn3T, "hn3T", i * 128,
                           (junk, st_ss[0 + b], st_ss[2 + b], st_ss[4 + b], hnb[b], f"p3cn{b}"), part=1)
            S.copy("dve", pbf[b], ptk[b], r=[f"ptk{b}"], w=[f"pbf{b}"])

        def ple_mid(i):
            b = i % 2
            norm_transpose(ht[b], [f"p3cht{b}"], 128, gain, gkey, hn3T, "hn3T", i * 128,
                           (junk, st_ss[0 + b], st_ss[2 + b], st_ss[4 + b], hnb[b], f"p3cn{b}"), part=2)
            pv = ps16(6)
            for k2 in range(2):
                S.tr(pv[:, k2 * 128:(k2 + 1) * 128], pbf[b][:, k2 * 128:(k2 + 1) * 128], ident,
                     r=[f"pbf{b}", "ident"], w=["ps6"])
            S.copy("act", pT[:, :, i * 128:(i + 1) * 128], pv[:, 0:256].rearrange("p (a b) -> p a b", b=128),
                   r=["ps6"], w=["pT"])

        ple_front(0)
        ple_mid(0)
        for i in range(NT):
            b = i % 2
            if i + 1 < NT:
                ple_front(i + 1)
            for nb in range(4):
                gbk = nb
                ebk = 4 + nb % 2
                for kc in range(KC):
                    S.mm(ps32(gbk), hn3T[:, kc, i * 128:(i + 1) * 128], wg[:, kc, nb * 512:(nb + 1) * 512],
                         start=(kc == 0), stop=(kc == KC - 1), r=["hn3T", "wg"], w=[f"ps{gbk}"] if kc in (0, KC - 1) else [])
                for k2 in range(2):
                    S.mm(ps32(ebk), pT[:, k2, i * 128:(i + 1) * 128], wp[:, k2, nb * 512:(nb + 1) * 512],
                         start=(k2 == 0), stop=(k2 == 1), r=["pT", "wp"], w=[f"ps{ebk}"])
                sg = sgs[nb % 2]
                S.act(sg, ps32(gbk), AF.Sigmoid, r=[f"ps{gbk}"], w=[f"sgs{nb % 2}"])
                S.tt("dve", egs[b][:, nb * 512:(nb + 1) * 512], ps32(ebk), sg, ALU.mult,
                     r=[f"ps{ebk}", f"sgs{nb % 2}"], w=[f"egs{b}"])
            if i + 1 < NT:
                ple_mid(i + 1)
            tag = f"p3ce{b}"
            ss, sd, rstd = st_s2[0 + b], st_s2[2 + b], st_s2[4 + b]
            S.act(junk2, egs[b], AF.Square, accum=ss, r=[f"egs{b}"], w=[tag + "junk", tag + "ss"])
            S.act(sd, ss, AF.Sqrt, bias=epsc, scale=1.0 / D, r=[tag + "ss", "epsc"], w=[tag + "sd"])
            S.op("dve", (lambda a_, b_: (lambda e: e.reciprocal(a_, b_)))(rstd, sd), [tag + "sd"], [tag + "rstd"])
            S.stt("dve", egs[b], egs[b], rstd, gainp, ALU.mult, ALU.mult, r=[f"egs{b}", tag + "rstd", gkeyp], w=[f"egs{b}"])
            S.tt("dve", egs[b], egs[b], ht[b], ALU.add, r=[f"egs{b}", f"p3cht{b}"], w=[f"egs{b}"])
            dst = out if l == DEPTH - 1 else hbuf
            S.dma("sp", dst[i * 128:(i + 1) * 128, :], egs[b], f"egs{b}", r=[f"egs{b}"], w=[f"hbuf{i}"])
        S.barrier()
        if upto < 6:
            break

    for name, (t, src) in dbg.items():
        S.dma("sp", t.ap(), src, "dbg_" + name, r=[], w=[])
    S.barrier()
    with ExitStack() as stack:
        S.replay(stack)
    return nc, list(dbg.keys())


def make_consts():
    idx = np.arange(128)
    c = {}
    c["c_ident"] = np.eye(128, dtype=np.float32)
    c["c_maskT"] = (idx[None, :] >= idx[:, None]).astype(np.float32)
    c["c_ones"] = np.ones((128, 128), np.float32)
    j = np.arange(32)
    c["c_striu"] = (j[:, None] < j[None, :]).astype(np.float32)
    seg = np.ones((128, 1024), np.float32)
    seg[:, ::64] = 0.0
    c["c_seg"] = seg
    return c


_CACHE = {}


def run(inputs, debug=False, upto=99, trace=False):
    key = (debug, upto)
    if key not in _CACHE:
        _CACHE[key] = build_program(debug=debug, upto=upto)
    nc, dbgnames = _CACHE[key]
    consts = make_consts()
    x = np.asarray(inputs["x"], np.float32)
    p = np.asarray(inputs["p"], np.float32)
    in_maps = []
    wnames = ["g_mix_pre", "w_in", "b_fox_f", "w_hgrn_lb", "g_hgrn_out", "w_out", "g_mix_post", "g_ffn_pre",
              "w_up", "conv_w", "conv_b", "w_down", "g_ffn_post", "g_ple_in", "w_ple_gate", "w_ple_proj", "g_ple_post"]
    shared = {n: np.ascontiguousarray(np.asarray(inputs[n], np.float32)) for n in wnames}
    for c in range(8):
        b, r = c // 4, c % 4
        m = dict(shared)
        m.update(consts)
        m["x"] = np.ascontiguousarray(x[b, r * NTOK:(r + 1) * NTOK, :])
        m["p"] = np.ascontiguousarray(p[:, b, r * NTOK:(r + 1) * NTOK, :])
        m["c_hmask"] = np.full((128, 1), 1.0 if r > 0 else 0.0, np.float32)
        sel = np.zeros((128, 4), np.float32)
        sel[:, r] = 1.0
        m["c_sel"] = sel
        in_maps.append(m)
    res = run_bass_kernel_spmd(nc, in_maps, core_ids=list(range(8)), trace=trace)
    return res, dbgnames


def kernel(**inputs):
    res, _ = run(inputs)
    outp = np.zeros((2, SEQ, D), np.float32)
    for c in range(8):
        b, r = c // 4, c % 4
        outp[b, r * NTOK:(r + 1) * NTOK, :] = res.results[c]["out"]
    return outp
```
